# Optimizing a Trainium2 kernel written in Bass

```python
import math
import numpy as np
import jax
import jax.numpy as jnp
from jax import lax

D_MODEL = 1024
BATCH = 4
SEQ = 4096
DEPTH = 2
DEC_BATCH = 32
DEC_SEQ = 64
PAST_LEN = 4096

CHUNK = 64
QBLOCK = 128
N_GROUPS = 4
GROUP_W = D_MODEL // N_GROUPS
N_HEADS = 4
HEAD_DIM = GROUP_W // N_HEADS
DIFF_HALF = HEAD_DIM // 2
CONV_W = 4
A_CONV_CH = 3 * GROUP_W
MEM_TOKENS = 256
MEM_HEADS = 4
MEM_HEAD_DIM = D_MODEL // MEM_HEADS
D_FF = ((8 * D_MODEL + 3 * 256 - 1) // (3 * 256)) * 256
EPS = 1e-6
NEG = -1e30
IN_SIZES = (3 * GROUP_W, GROUP_W, N_HEADS, N_HEADS,
            GROUP_W, GROUP_W, GROUP_W,
            GROUP_W, GROUP_W, GROUP_W,
            GROUP_W, GROUP_W, GROUP_W, GROUP_W)
D_IN = 14 * GROUP_W + 2 * N_HEADS

kernel_name = "hybrid_streaming_parallel_groups_step"


def rmsnorm(x, w):
    xf = x.astype(jnp.float32)
    y = xf * lax.rsqrt(jnp.mean(xf * xf, axis=-1, keepdims=True) + EPS)
    return (y * w.astype(jnp.float32)).astype(x.dtype)


def l2norm(x):
    xf = x.astype(jnp.float32)
    return (xf * lax.rsqrt(jnp.sum(xf * xf, axis=-1, keepdims=True) + EPS)).astype(x.dtype)


def gated_rmsnorm(o, w, z):
    of = o.astype(jnp.float32)
    y = of * lax.rsqrt(jnp.mean(of * of, axis=-1, keepdims=True) + EPS) * w.astype(jnp.float32)
    return (y * jax.nn.silu(z.astype(jnp.float32))).astype(z.dtype)


def _chunks(t, L):
    b, n = t.shape[0], t.shape[1] // L
    t = t.reshape((b, n, L) + t.shape[2:])
    return jnp.moveaxis(jnp.moveaxis(t, 1, 0), 2, 3)


def _unchunk(o, b, t):
    o = jnp.moveaxis(jnp.moveaxis(o, 3, 2), 0, 1)
    return o.reshape((b, t) + o.shape[3:])


def _masked_decay(diff, mask):
    return jnp.where(mask, jnp.exp(jnp.where(mask, diff, 0.0)), 0.0)


def gated_delta_chunked(q, k, v, g, beta, s0):
    b, t = q.shape[0], q.shape[1]
    L = min(CHUNK, t)
    f32 = jnp.float32
    incl = jnp.tril(jnp.ones((L, L), bool))
    strict = jnp.tril(jnp.ones((L, L), bool), -1)
    eye = jnp.eye(L, dtype=f32)

    def step(S, inp):
        qc, kc, vc, gc, bc = inp
        G = jnp.cumsum(gc, axis=-1)
        decay = _masked_decay(G[..., :, None] - G[..., None, :], incl)
        kb = kc * bc[..., None]
        a = jnp.where(strict, jnp.einsum("bhid,bhjd->bhij", kb, kc) * decay, 0.0)
        rhs = vc * bc[..., None] - jnp.einsum("bhid,bhde->bhie", kb * jnp.exp(G)[..., None], S)
        w = lax.linalg.triangular_solve(eye + a, rhs, left_side=True, lower=True, unit_diagonal=True)
        o = (jnp.einsum("bhid,bhde->bhie", qc * jnp.exp(G)[..., None], S)
             + jnp.einsum("bhij,bhje->bhie", jnp.einsum("bhid,bhjd->bhij", qc, kc) * decay, w))
        gl = G[..., -1]
        S_new = (S * jnp.exp(gl)[..., None, None]
                 + jnp.einsum("bhjd,bhje->bhde", kc * jnp.exp(gl[..., None] - G)[..., None], w))
        return S_new, o

    xs = tuple(_chunks(a.astype(f32), L) for a in (q, k, v, g, beta))
    S, o = lax.scan(step, s0.astype(f32), xs)
    return _unchunk(o, b, t), S


def hgrn2_chunked(q, k, v, log_f, s0):
    b, t = q.shape[0], q.shape[1]
    L = min(CHUNK, t)
    f32 = jnp.float32
    incl = jnp.tril(jnp.ones((L, L), bool))[..., None]

    def step(S, inp):
        qc, kc, vc, lf = inp
        Bc = jnp.cumsum(lf, axis=2)
        diff = Bc[:, :, :, None, :] - Bc[:, :, None, :, :]
        decay = _masked_decay(diff, incl)
        a = jnp.einsum("bhid,bhjd,bhijd->bhij", qc, kc, decay)
        o = jnp.einsum("bhid,bhde->bhie", qc * jnp.exp(Bc), S) + jnp.einsum("bhij,bhje->bhie", a, vc)
        bl = Bc[:, :, -1]
        S_new = S * jnp.exp(bl)[..., None] + jnp.einsum("bhjd,bhje->bhde", kc * jnp.exp(bl[:, :, None] - Bc), vc)
        return S_new, o

    xs = tuple(_chunks(a.astype(f32), L) for a in (q, k, v, log_f))
    S, o = lax.scan(step, s0.astype(f32), xs)
    return _unchunk(o, b, t), S


def sweep_queries(fn, q, pos_q):
    b, t = q.shape[0], q.shape[1]
    if t <= QBLOCK:
        return fn(q, pos_q)
    nb = t // QBLOCK
    qb = jnp.moveaxis(q.reshape((b, nb, QBLOCK) + q.shape[2:]), 1, 0)
    pb = pos_q.reshape(nb, QBLOCK)
    out = lax.map(lambda qp: fn(qp[0], qp[1]), (qb, pb))
    out = jnp.moveaxis(out, 0, 1)
    return out.reshape((b, t) + out.shape[3:])


def diff_attn_block(qb, pq, k, v, pk, lam, slopes):
    b, tq, h, _ = qb.shape
    q2 = qb.reshape(b, tq, h, 2, DIFF_HALF)
    k2 = k.reshape(k.shape[0], k.shape[1], h, 2, DIFF_HALF)
    s = jnp.einsum("bqhcd,bkhcd->bchqk", q2, k2).astype(jnp.float32) * DIFF_HALF ** -0.5
    rel = jnp.abs(pq[:, None] - pk[None, :]).astype(jnp.float32)
    allowed = (pk[None, :] // CHUNK) <= (pq[:, None] // CHUNK)
    s = jnp.where(allowed, s - slopes[:, None, None] * rel, NEG)
    p = jax.nn.softmax(s, axis=-1)
    w = p[:, 0] - lam * p[:, 1]
    return jnp.einsum("bhqk,bkhe->bqhe", w.astype(v.dtype), v)


def stick_breaking_block(qb, pq, k, v, pk):
    z = jnp.einsum("bqhd,bkhd->bhqk", qb, k).astype(jnp.float32) * HEAD_DIM ** -0.5
    mask = pk[None, :] < pq[:, None]
    log_keep = jnp.where(mask, jax.nn.log_sigmoid(-z), 0.0)
    after = lax.cumsum(log_keep, axis=3, reverse=True) - log_keep
    a = jnp.where(mask, jnp.exp(jnp.where(mask, jax.nn.log_sigmoid(z) + after, 0.0)), 0.0)
    return jnp.einsum("bhqk,bkhe->bqhe", a.astype(v.dtype), v)


def hgrn_lower_bounds(lb_raw):
    p = jax.nn.softmax(lb_raw.astype(jnp.float32), axis=0)
    return jnp.cumsum(p, axis=0) - p[0]


def _layer(x, l, st, mk, mv, W, lb, slopes):
    b, t, _ = x.shape
    dt = x.dtype
    f32 = jnp.float32
    heads = lambda a: a.reshape(b, t, N_HEADS, HEAD_DIM)
    past = 0 if st["b_k"] is None else st["b_k"].shape[1]
    pos_q = past + jnp.arange(t, dtype=jnp.int32)
    pos_k = jnp.arange(past + t, dtype=jnp.int32)
    with_past = lambda old, new: new if old is None else jnp.concatenate([old.astype(dt), new], axis=1)

    h = rmsnorm(x, W["norm_mix"][l])
    split_idx = np.cumsum(IN_SIZES)[:-1].tolist()
    (a_qkv, a_z, a_b, a_g, b_q, b_k, b_v, c_q, c_k, c_v,
     d_q, d_f, d_i, d_g) = jnp.split(h @ W["w_in"][l], split_idx, axis=-1)

    conv_in = jnp.concatenate([st["a_conv"].astype(dt), a_qkv], axis=1)
    u = lax.conv_general_dilated(conv_in, W["a_conv_w"][l][:, None, :].astype(dt), (1,), "VALID",
                                 dimension_numbers=("NWC", "WIO", "NWC"), feature_group_count=A_CONV_CH)
    aq, ak, av = (heads(a) for a in jnp.split(jax.nn.silu(u), 3, axis=-1))
    beta = jax.nn.sigmoid(a_b.astype(f32))
    g = -jnp.exp(W["a_A_log"][l].astype(f32)) * jax.nn.softplus(a_g.astype(f32) + W["a_dt_bias"][l].astype(f32))
    o_a, a_S = gated_delta_chunked(l2norm(aq) * HEAD_DIM ** -0.5, l2norm(ak), av, g, beta, st["a_S"])
    o_a = gated_rmsnorm(o_a, W["a_norm"][l], heads(a_z))

    bk, bv = heads(b_k), heads(b_v)
    lam_init = 0.8 - 0.6 * math.exp(-0.3 * l)
    lam = (jnp.exp(jnp.sum(W["b_lam_q1"][l].astype(f32) * W["b_lam_k1"][l].astype(f32)))
           - jnp.exp(jnp.sum(W["b_lam_q2"][l].astype(f32) * W["b_lam_k2"][l].astype(f32))) + lam_init)
    kb_all, vb_all = with_past(st["b_k"], bk), with_past(st["b_v"], bv)
    o_b = sweep_queries(lambda qb, pq: diff_attn_block(qb, pq, kb_all, vb_all, pos_k, lam, slopes), heads(b_q), pos_q)
    o_b = (rmsnorm(o_b, W["b_norm"][l]).astype(f32) * (1.0 - lam_init)).astype(dt)

    ck, cv = heads(c_k), heads(c_v)
    kc_all, vc_all = with_past(st["c_k"], ck), with_past(st["c_v"], cv)
    o_c = sweep_queries(lambda qb, pq: stick_breaking_block(qb, pq, kc_all, vc_all, pos_k), heads(c_q), pos_q)

    lbh = lb[l].reshape(N_HEADS, HEAD_DIM)
    fl = heads(d_f).astype(f32)
    log_f = jax.nn.log_sigmoid(fl) + jnp.log1p(lbh * jnp.exp(-fl))
    d_key = (1.0 - lbh) * jax.nn.sigmoid(-fl)
    o_d, d_S = hgrn2_chunked(heads(d_q) * HEAD_DIM ** -0.5, d_key, heads(d_i), log_f, st["d_S"])
    o_d = gated_rmsnorm(o_d, W["d_norm"][l], heads(d_g))

    o_mix = jnp.concatenate([o_a, o_b.astype(dt), o_c.astype(dt), o_d], axis=2).reshape(b, t, N_GROUPS * GROUP_W)
    x = x + o_mix @ W["w_out"][l]

    cq = (rmsnorm(x, W["norm_cross"][l]) @ W["w_cq"][l]).reshape(b, t, MEM_HEADS, MEM_HEAD_DIM)
    s = jnp.einsum("bthd,bmhd->bhtm", cq, mk.astype(dt)).astype(f32) * MEM_HEAD_DIM ** -0.5
    p = jax.nn.softmax(s, axis=-1).astype(dt)
    co = jnp.einsum("bhtm,bmhd->bthd", p, mv.astype(dt)).reshape(b, t, D_MODEL)
    x = x + co @ W["w_co"][l]

    hf = rmsnorm(x, W["norm_ffn"][l])
    x = x + (jax.nn.silu(hf @ W["w_gate"][l]) * (hf @ W["w_up"][l])) @ W["w_down"][l]

    new = (conv_in[:, -(CONV_W - 1):], a_S.astype(dt), bk, bv, ck, cv, d_S.astype(dt))
    return x, new


def setup_inputs(seed: int = 0) -> dict:
    key = jax.random.key(seed)
    keys = iter(jax.random.split(key, 48))
    f32 = jnp.float32

    def nrm(shape, scale):
        return jax.random.normal(next(keys), shape, f32) * scale

    def gain(shape):
        return 1.0 + nrm(shape, 0.02)

    a_A = jax.random.uniform(next(keys), (DEPTH, N_HEADS), f32, 1.0, 16.0)
    a_dt = jnp.exp(jax.random.uniform(next(keys), (DEPTH, N_HEADS), f32, math.log(1e-3), math.log(1e-1)))
    kv_shape = (DEPTH, DEC_BATCH, PAST_LEN, N_HEADS, HEAD_DIM)
    mem_shape = (DEPTH, DEC_BATCH, MEM_TOKENS, MEM_HEADS, MEM_HEAD_DIM)
    return {
        "x_prompt": nrm((BATCH, SEQ, D_MODEL), 1.0),
        "x_sample": nrm((DEC_BATCH, DEC_SEQ, D_MODEL), 1.0),
        "mem_prompt": nrm((BATCH, MEM_TOKENS, D_MODEL), 1.0),
        "state_a_conv": nrm((DEPTH, DEC_BATCH, CONV_W - 1, A_CONV_CH), 1.0),
        "state_a_S": nrm((DEPTH, DEC_BATCH, N_HEADS, HEAD_DIM, HEAD_DIM), 0.1),
        "cache_b_k": nrm(kv_shape, 1.0),
        "cache_b_v": nrm(kv_shape, 1.0),
        "cache_c_k": nrm(kv_shape, 1.0),
        "cache_c_v": nrm(kv_shape, 1.0),
        "state_d_S": nrm((DEPTH, DEC_BATCH, N_HEADS, HEAD_DIM, HEAD_DIM), 0.3),
        "cache_mem_k": nrm(mem_shape, 1.0),
        "cache_mem_v": nrm(mem_shape, 1.0),
        "norm_mix": gain((DEPTH, D_MODEL)),
        "w_in": nrm((DEPTH, D_MODEL, D_IN), D_MODEL ** -0.5),
        "a_conv_w": nrm((DEPTH, CONV_W, A_CONV_CH), CONV_W ** -0.5),
        "a_A_log": jnp.log(a_A),
        "a_dt_bias": jnp.log(jnp.expm1(a_dt)),
        "a_norm": gain((DEPTH, HEAD_DIM)),
        "b_lam_q1": nrm((DEPTH, DIFF_HALF), 0.1),
        "b_lam_k1": nrm((DEPTH, DIFF_HALF), 0.1),
        "b_lam_q2": nrm((DEPTH, DIFF_HALF), 0.1),
        "b_lam_k2": nrm((DEPTH, DIFF_HALF), 0.1),
        "b_norm": gain((DEPTH, HEAD_DIM)),
        "d_lb": nrm((DEPTH, GROUP_W), 0.1),
        "d_norm": gain((DEPTH, HEAD_DIM)),
        "w_out": nrm((DEPTH, N_GROUPS * GROUP_W, D_MODEL), (N_GROUPS * GROUP_W) ** -0.5),
        "norm_cross": gain((DEPTH, D_MODEL)),
        "norm_memtok": gain((DEPTH, D_MODEL)),
        "w_cq": nrm((DEPTH, D_MODEL, D_MODEL), D_MODEL ** -0.5),
        "w_ck": nrm((DEPTH, D_MODEL, D_MODEL), D_MODEL ** -0.5),
        "w_cv": nrm((DEPTH, D_MODEL, D_MODEL), D_MODEL ** -0.5),
        "w_co": nrm((DEPTH, D_MODEL, D_MODEL), D_MODEL ** -0.5),
        "norm_ffn": gain((DEPTH, D_MODEL)),
        "w_gate": nrm((DEPTH, D_MODEL, D_FF), D_MODEL ** -0.5),
        "w_up": nrm((DEPTH, D_MODEL, D_FF), D_MODEL ** -0.5),
        "w_down": nrm((DEPTH, D_FF, D_MODEL), D_FF ** -0.5),
        "norm_final": gain((D_MODEL,)),
    }


def reference(x_prompt, x_sample, mem_prompt, state_a_conv, state_a_S, cache_b_k, cache_b_v,
              cache_c_k, cache_c_v, state_d_S, cache_mem_k, cache_mem_v,
              norm_mix, w_in, a_conv_w, a_A_log, a_dt_bias, a_norm, b_lam_q1, b_lam_k1, b_lam_q2,
              b_lam_k2, b_norm, d_lb, d_norm, w_out, norm_cross, norm_memtok, w_cq, w_ck, w_cv, w_co,
              norm_ffn, w_gate, w_up, w_down, norm_final):
    W = {"norm_mix": norm_mix, "w_in": w_in, "a_conv_w": a_conv_w, "a_A_log": a_A_log,
         "a_dt_bias": a_dt_bias, "a_norm": a_norm, "b_lam_q1": b_lam_q1, "b_lam_k1": b_lam_k1,
         "b_lam_q2": b_lam_q2, "b_lam_k2": b_lam_k2, "b_norm": b_norm, "d_norm": d_norm,
         "w_out": w_out, "norm_cross": norm_cross, "w_cq": w_cq, "w_co": w_co,
         "norm_ffn": norm_ffn, "w_gate": w_gate, "w_up": w_up, "w_down": w_down}
    lb = hgrn_lower_bounds(d_lb)
    slopes = jnp.exp2(-8.0 * jnp.arange(1, N_HEADS + 1, dtype=jnp.float32) / N_HEADS)

    bp, dt = x_prompt.shape[0], x_prompt.dtype
    h = x_prompt
    p_new = []
    for l in range(DEPTH):
        mem_h = rmsnorm(mem_prompt, norm_memtok[l])
        mk = (mem_h @ w_ck[l]).reshape(bp, -1, MEM_HEADS, MEM_HEAD_DIM)
        mv = (mem_h @ w_cv[l]).reshape(bp, -1, MEM_HEADS, MEM_HEAD_DIM)
        st = {"a_conv": jnp.zeros((bp, CONV_W - 1, A_CONV_CH), dt),
              "a_S": jnp.zeros((bp, N_HEADS, HEAD_DIM, HEAD_DIM), jnp.float32),
              "b_k": None, "b_v": None, "c_k": None, "c_v": None,
              "d_S": jnp.zeros((bp, N_HEADS, HEAD_DIM, HEAD_DIM), jnp.float32)}
        h, new = _layer(h, l, st, mk, mv, W, lb, slopes)
        p_new.append(new + (mk, mv))
    y_prompt = rmsnorm(h, norm_final)
    p_a_conv, p_a_S, p_b_k, p_b_v, p_c_k, p_c_v, p_d_S, p_mem_k, p_mem_v = [jnp.stack(c) for c in zip(*p_new)]

    h = x_sample
    s_new = []
    for l in range(DEPTH):
        st = {"a_conv": state_a_conv[l], "a_S": state_a_S[l], "b_k": cache_b_k[l], "b_v": cache_b_v[l],
              "c_k": cache_c_k[l], "c_v": cache_c_v[l], "d_S": state_d_S[l]}
        h, new = _layer(h, l, st, cache_mem_k[l], cache_mem_v[l], W, lb, slopes)
        s_new.append(new)
    y_sample = rmsnorm(h, norm_final)
    s_a_conv, s_a_S, s_b_k, s_b_v, s_c_k, s_c_v, s_d_S = [jnp.stack(c) for c in zip(*s_new)]

    return (y_prompt, y_sample, p_a_conv, p_a_S, p_b_k, p_b_v, p_c_k, p_c_v, p_d_S, p_mem_k, p_mem_v,
            s_a_conv, s_a_S, s_b_k, s_b_v, s_c_k, s_c_v, s_d_S)
```

```python
import math
import numpy as np
import concourse.bass as bass
import concourse.mybir as mybir
from concourse.bass_utils import run_bass_kernel_spmd

F32 = mybir.dt.float32
BF16 = mybir.dt.bfloat16
AF = mybir.ActivationFunctionType
ALU = mybir.AluOpType
AX = mybir.AxisListType

D = 1024
DFF = 2816
DIN = 3592
NCORES = 8
EPS = 1e-6
SLOPES = [2.0 ** (-8.0 * (h + 1) / 4) for h in range(4)]


class Buf:
    __slots__ = ("ap", "name", "w", "r", "dsem", "dcnt", "lastw", "excl")

    def __init__(self, ap, name, excl=False):
        self.excl = excl
        self.ap = ap
        self.name = name
        self.w = None
        self.r = {}
        self.dsem = None
        self.dcnt = 0

    def __getitem__(self, k):
        return self.ap[k]


class Rec:
    ENG = ("pe", "act", "dve", "pool", "sp")

    def __init__(self, nc):
        self.nc = nc
        self.ops = []
        self.last = {e: None for e in self.ENG}
        self.dbufs = []
        self.pending_barrier = {e: [] for e in self.ENG}

    def _deps(self, reads, writes):
        deps = set()
        for b in reads:
            if b.w is not None:
                deps.add(b.w)
        for b in writes:
            if b.w is not None:
                deps.add(b.w)
            deps.update(b.r.values())
        return deps

    def op(self, eng, fn, reads=(), writes=(), dmabuf=None):
        if any(b.excl for b in reads):
            writes = tuple(writes) + tuple(b for b in reads if b.excl)
            reads = tuple(b for b in reads if not b.excl)
        i = len(self.ops)
        deps = self._deps(reads, writes)
        if self.pending_barrier[eng]:
            deps.update(self.pending_barrier[eng])
            self.pending_barrier[eng] = []
        dval = None
        if dmabuf is not None:
            if dmabuf.dsem is None:
                dmabuf.dsem = True
                self.dbufs.append(dmabuf)
            dmabuf.dcnt += 16
            dval = dmabuf.dcnt
        dl = []
        for j in deps:
            pj = self.ops[j]
            if pj[3] is not None:
                dl.append((j, pj[3], pj[3].dcnt if pj[3] is not dmabuf else pj[3].dcnt - 16))
            else:
                dl.append((j, None, 0))
        self.ops.append((eng, fn, dl, dmabuf, dval))
        for b in reads:
            b.r[eng] = i
        for b in writes:
            b.w = i
            b.r = {}
        self.last[eng] = i
        return i

    def barrier(self):
        lasts = [v for v in self.last.values() if v is not None]
        for e in self.ENG:
            self.pending_barrier[e] = list(lasts) + [b.lastw for b in self.dbufs if getattr(b, "lastw", None) is not None]

    def replay(self):
        nc = self.nc
        ops = self.ops
        n = len(ops)
        need_inc = [False] * n
        for (eng, fn, dl, dmabuf, dval) in ops:
            for (j, db, dv) in dl:
                if db is None and (ops[j][0] != eng or dmabuf is not None or eng != "pe"):
                    need_inc[j] = True
        inc_idx = [0] * n
        cnt = {e: 0 for e in self.ENG}
        for i, o in enumerate(ops):
            if need_inc[i]:
                cnt[o[0]] += 1
                inc_idx[i] = cnt[o[0]]
        import contextlib
        with contextlib.ExitStack() as st:
            esem = {e: st.enter_context(nc.semaphore("sem_" + e)) for e in self.ENG}
            for k, b in enumerate(self.dbufs):
                b.dsem = st.enter_context(nc.semaphore("dsem%d" % k))
            block = st.enter_context(nc.Block())
            per_eng = {e: [i for i, o in enumerate(ops) if o[0] == e] for e in self.ENG}
            dbufs = self.dbufs

            def run(e, ename):
                waited = {x: 0 for x in self.ENG}
                dwaited = {}
                for i in per_eng[ename]:
                    (_, fn, dl, dmabuf, dval) = ops[i]
                    for (j, db, dv) in dl:
                        if db is not None:
                            if dwaited.get(id(db), 0) < dv:
                                e.wait_ge(db.dsem, dv)
                                dwaited[id(db)] = dv
                        else:
                            pe_ = ops[j][0]
                            if (pe_ != ename or dmabuf is not None or ename != "pe") and waited[pe_] < inc_idx[j]:
                                e.wait_ge(esem[pe_], inc_idx[j])
                                waited[pe_] = inc_idx[j]
                    ins = fn(e)
                    if dmabuf is not None:
                        ins.then_inc(dmabuf.dsem, 16)
                    elif need_inc[i]:
                        ins.then_inc(esem[ename], 1)
                if ename == "sp":
                    for b in dbufs:
                        if b.dcnt:
                            e.wait_ge(b.dsem, b.dcnt)

            @block.tensor
            def _(e):
                run(e, "pe")

            @block.scalar
            def _(e):
                run(e, "act")

            @block.vector
            def _(e):
                run(e, "dve")

            @block.gpsimd
            def _(e):
                run(e, "pool")

            @block.sync
            def _(e):
                run(e, "sp")


class K:
    def __init__(self, cfg):
        self.cfg = cfg
        self.nc = bass.Bass("TRN2", target_bir_lowering=False)
        self.R = Rec(self.nc)
        self.din = {}
        self.dout = {}
        self._uid = 0

    def inp(self, name, shape):
        self.din[name] = self.nc.dram_tensor(name, list(shape), F32, kind="ExternalInput").ap()
        return self.din[name]

    def outp(self, name, shape):
        self.dout[name] = self.nc.dram_tensor(name, list(shape), F32, kind="ExternalOutput").ap()
        return self.dout[name]

    def init_mem(self, st):
        nc = self.nc
        self.ARW = 53000
        self.arena = st.enter_context(nc.sbuf_tensor("arena", [128, self.ARW], F32))
        self.top = 0
        self.psb = [st.enter_context(nc.psum_tensor("psb%d" % i, [128, 512], F32)) for i in range(8)]
        self.PS = [Buf(self.psb[i], "ps%d" % i, excl=True) for i in range(8)]

    def alloc(self, name, free_shape, dt=F32, parts=128):
        nel = int(np.prod(free_shape))
        nw = (nel * (4 if dt == F32 else 2) + 3) // 4
        nw = (nw + 7) // 8 * 8
        off = self.top
        self.top += nw
        assert self.top <= self.ARW, ("SBUF overflow", name, self.top)
        ap = self.arena[0:parts, off:off + nw]
        if dt != F32:
            ap = ap.bitcast(dt)
        ap = ap[:, 0:nel]
        if len(free_shape) == 2:
            ap = ap.rearrange("p (a b) -> p a b", b=free_shape[1])
        elif len(free_shape) == 3:
            ap = ap.rearrange("p (a b c) -> p a b c", b=free_shape[1], c=free_shape[2])
        elif len(free_shape) == 4:
            ap = ap.rearrange("p (a b c d) -> p a b c d", b=free_shape[1], c=free_shape[2], d=free_shape[3])
        return Buf(ap, name)

    def psv(self, i, free_shape, dt=F32, parts=128):
        ap = self.psb[i][0:parts, :]
        if dt != F32:
            ap = ap.bitcast(dt)
        nel = int(np.prod(free_shape))
        ap = ap[:, 0:nel]
        if len(free_shape) == 2:
            ap = ap.rearrange("p (a b) -> p a b", b=free_shape[1])
        elif len(free_shape) == 3:
            ap = ap.rearrange("p (a b c) -> p a b c", b=free_shape[1], c=free_shape[2])
        return ap

    def mm(self, out, lhsT, rhs, start=True, stop=True, R=(), W=()):
        self.R.op("pe", lambda e: e.matmul(out, lhsT=lhsT, rhs=rhs, start=start, stop=stop), R, W)

    def tr(self, out, in_, ident, R=(), W=()):
        self.R.op("pe", lambda e: e.transpose(out, in_, ident), R, W)

    def act(self, out, in_, func, bias=0.0, scale=1.0, R=(), W=(), accum=None):
        if accum is None:
            self.R.op("act", lambda e: e.activation(out=out, in_=in_, func=func, bias=bias, scale=scale), R, W)
        else:
            self.R.op("act", lambda e: e.activation(out=out, in_=in_, func=func, bias=bias, scale=scale,
                                                    accum_out=accum), R, W)

    def ts(self, out, in0, s1, s2, op0, op1=None, R=(), W=(), eng="dve"):
        if op1 is None:
            self.R.op(eng, lambda e: e.tensor_scalar(out=out, in0=in0, scalar1=s1, scalar2=None, op0=op0), R, W)
        else:
            self.R.op(eng, lambda e: e.tensor_scalar(out=out, in0=in0, scalar1=s1, scalar2=s2, op0=op0, op1=op1), R, W)

    def tt(self, out, in0, in1, op, R=(), W=(), eng="dve"):
        self.R.op(eng, lambda e: e.tensor_tensor(out=out, in0=in0, in1=in1, op=op), R, W)

    def stt(self, out, in0, scalar, in1, op0, op1, R=(), W=()):
        self.R.op("dve", lambda e: e.scalar_tensor_tensor(out=out, in0=in0, scalar=scalar, in1=in1, op0=op0, op1=op1),
                  R, W)

    def cp(self, out, in_, R=(), W=(), eng="dve"):
        if eng == "act":
            self.R.op("act", lambda e: e.activation(out=out, in_=in_, func=AF.Copy), R, W)
        else:
            self.R.op(eng, lambda e: e.tensor_copy(out=out, in_=in_), R, W)

    def memset(self, ap, val, W=(), eng="pool"):
        self.R.op(eng, lambda e: e.memset(ap, val), (), W)

    def red(self, out, in_, op=ALU.add, R=(), W=()):
        self.R.op("dve", lambda e: e.tensor_reduce(out=out, in_=in_, axis=AX.X, op=op), R, W)

    def dma(self, q, out, in_, buf, R=(), W=(), slow=False):
        if slow:
            i = self.R.op(q, lambda e: e.dma_start(out=out, in_=in_, allow_slow_non_contiguous=True), R, W, dmabuf=buf)
        else:
            i = self.R.op(q, lambda e: e.dma_start(out=out, in_=in_), R, W, dmabuf=buf)
        buf.lastw = i

    def load(self, buf, out, in_, q="sp", slow=False):
        self.dma(q, out, in_, buf, (), (buf,), slow=slow)

    def store(self, buf, out, in_, q="sp"):
        self.dma(q, out, in_, buf, (buf,), ())

    @staticmethod
    def host_consts():
        p = np.arange(128)[:, None].astype(np.float64)
        f = np.arange(128)[None, :].astype(np.float64)
        cf = {}
        cf["ident"] = (p == f).astype(np.float32)
        cf["cmask"] = (p < f).astype(np.float32)
        dg = np.zeros((128, 4, 128), np.float32)
        for h in range(4):
            b = -SLOPES[h] * np.abs(f - p) + SLOPES[h] * f
            b = np.where((p >= 64) & (f < 64), -30000.0, b)
            dg[:, h, :] = b
        cf["diagb"] = dg.reshape(128, 512)
        ds_ = np.zeros((128, 4, 64), np.float32)
        pl = (np.arange(128) % 64)[:, None].astype(np.float64)
        f64 = np.arange(64)[None, :].astype(np.float64)
        for h in range(4):
            ds_[:, h, :] = -SLOPES[h] * np.abs(f64 - pl) + SLOPES[h] * f64
        cf["diags"] = ds_.reshape(128, 256)
        bs = np.zeros((128, 4, 33), np.float32)
        for h in range(4):
            for n in range(33):
                bs[:, h, n] = SLOPES[h] * (np.arange(128) - 128.0 * n)
        cf["bias"] = bs.reshape(128, 132)
        j = (np.arange(128) % 64)[:, None]
        i = np.arange(64)[None, :]
        cf["tri64"] = (j <= i).astype(np.float32)
        cf["mneg_ui"] = np.tile(np.where(j <= i, 0.0, -30000.0).astype(np.float32)[:, None, :], (1, 4, 1)).reshape(128, 256)
        cf["mpos_sl"] = np.tile(np.where(i < j, 0.0, 30000.0).astype(np.float32)[:, None, :], (1, 4, 1)).reshape(128, 256)
        cf["mask_ui"] = np.tile((j <= i).astype(np.float32)[:, None, :], (1, 4, 1)).reshape(128, 256)
        cf["blkones"] = ((np.arange(128)[:, None] // 64) == (np.arange(128)[None, :] // 64)).astype(np.float32)
        cf["ones"] = np.ones((128, 128), np.float32)
        cf["ident64"] = np.tile(np.eye(64, dtype=np.float32), (2, 1))
        cf["epsc"] = np.full((128, 2), EPS, np.float32)
        cb = {}
        cb["identb"] = cf["ident"]
        cb["utri"] = (p >= f).astype(np.float32)
        cb["onesb"] = np.ones((128, 128), np.float32)
        cb["zerob"] = np.zeros((128, 512), np.float32)
        return cf, cb

    def load_consts(self):
        cf, cb = self.host_consts()
        self.cf_off, o = {}, 0
        for k, v in cf.items():
            self.cf_off[k] = (o, v.shape[1])
            o += v.shape[1]
        self.ncf = o
        self.cb_off, o = {}, 0
        for k, v in cb.items():
            self.cb_off[k] = (o, v.shape[1])
            o += v.shape[1]
        self.ncb = o
        dcf = self.inp("constf", [128, self.ncf])
        dcb = self.inp("constb", [128, self.ncb])
        self.CF = self.alloc("CF", [self.ncf])
        self.CB = self.alloc("CB", [self.ncb], BF16)
        self.load(self.CF, self.CF[:, :], dcf[:, :])
        self.load(self.CB, self.CB[:, :], dcb[:, :], q="pool")
        self.epsb = self.c("epsc")
        ns = self.small_layout()
        dsm = self.inp("smallp", [128, ns])
        self.SM = self.alloc("SM", [ns])
        self.load(self.SM, self.SM[:, :], dsm[:, :])

    def c(self, name, parts=128, p0=0):
        o, n = self.cf_off[name]
        return self.CF[p0:p0 + parts, o:o + n]

    def cbf(self, name, parts=128, p0=0):
        o, n = self.cb_off[name]
        return self.CB[p0:p0 + parts, o:o + n]

    SMALL = [("nw", 8 * 8), ("convw", 2 * 6 * 4), ("dlb", 2 * 2), ("anorm", 2 * 256), ("bnorm", 2 * 256),
             ("dnorm", 2 * 256), ("alog", 2 * 4), ("dtb", 2 * 4), ("lamv", 2 * 4 * 32)]

    def small_layout(self):
        self.sm_off, o = {}, 0
        for k, n in self.SMALL:
            self.sm_off[k] = (o, n)
            o += n
        return o

    @classmethod
    def host_small(cls, inp):
        out = []
        norms = [inp["norm_mix"][0], inp["norm_mix"][1], inp["norm_cross"][0], inp["norm_cross"][1],
                 inp["norm_ffn"][0], inp["norm_ffn"][1], inp["norm_memtok"][0], inp["norm_memtok"][1]]
        nw = np.stack([n.reshape(8, 128).T for n in norms], axis=1)
        out.append(nw.reshape(128, 64))
        cw = inp["a_conv_w"].reshape(2, 4, 6, 128).transpose(3, 0, 2, 1)
        out.append(cw.reshape(128, 48))
        dl = inp["d_lb"].reshape(2, 2, 128).transpose(2, 0, 1)
        out.append(dl.reshape(128, 4))
        for nm in ("a_norm", "b_norm", "d_norm"):
            v = np.tile(inp[nm][:, None, :], (1, 4, 1)).reshape(1, 512)
            out.append(np.broadcast_to(v, (128, 512)))
        out.append(np.broadcast_to(inp["a_A_log"].reshape(1, 8), (128, 8)))
        out.append(np.broadcast_to(inp["a_dt_bias"].reshape(1, 8), (128, 8)))
        lv = np.stack([inp["b_lam_q1"], inp["b_lam_k1"], inp["b_lam_q2"], inp["b_lam_k2"]], axis=1)
        out.append(np.broadcast_to(lv.reshape(1, 256), (128, 256)))
        return np.ascontiguousarray(np.concatenate(out, axis=1).astype(np.float32))

    def sm(self, name, parts=128, p0=0):
        o, n = self.sm_off[name]
        return self.SM[p0:p0 + parts, o:o + n]

    def wload(self, W, dram2d, nk, q="pool"):
        for k in range(nk):
            self.load(W, W[:, k, :], dram2d[k * 128:(k + 1) * 128, :], q=q)

    def norm_T(self, x_ap, xbuf, nw_idx, hT, tcols=slice(0, 128), psb=0):
        XN, SS = self.XN, self.SS
        PSb = self.PS[psb]
        self.act(XN[:, :], x_ap, AF.Square, R=(xbuf,), W=(XN, SS), accum=SS[:, 0:1])
        self.act(SS[:, 1:2], SS[:, 0:1], AF.Ln, bias=self.epsb[:, 0:1], scale=1.0 / D, R=(SS, self.CF), W=(SS,))
        self.act(SS[:, 2:3], SS[:, 1:2], AF.Exp, scale=-0.5, R=(SS,), W=(SS,))
        self.ts(XN[:, :], x_ap, SS[:, 2:3], None, ALU.mult, R=(xbuf, SS), W=(XN,))
        pv = self.psv(psb, [8, 128], BF16)
        for k in range(8):
            self.tr(pv[:, k, :], XN[:, k * 128:(k + 1) * 128], self.cbf("identb"), R=(XN, self.CB), W=(PSb,))
        nw = self.sm("nw")
        for k in range(8):
            o = nw_idx * 8 + k
            self.act(hT[:, k, tcols], pv[:, k, :], AF.Copy, scale=nw[:, o:o + 1], R=(PSb, self.SM), W=(hT,))

    def xrows(self, t):
        ntp = self.cfg["NTP"]
        if t < ntp:
            return self.dout["y_p"][t * 128:(t + 1) * 128, :]
        return self.dout["y_s"][(t - ntp) * 128:(t - ntp + 1) * 128, :]

    def xin_rows(self, t):
        ntp = self.cfg["NTP"]
        if t < ntp:
            return self.din["xp"][t * 128:(t + 1) * 128, :]
        return self.din["xs"][(t - ntp) * 128:(t - ntp + 1) * 128, :]

    def bmul(self, out, in0, sc, nh, R=(), W=()):
        for h in range(nh):
            self.ts(out[:, h, :], in0[:, h, :], sc[:, h:h + 1], None, ALU.mult, R=R, W=W)

    def rsqrt_act(self, out, in_, scale, R=(), W=()):
        P = out.shape[0]
        self.act(out, in_, AF.Ln, bias=self.epsb[0:P, 0:1], scale=scale, R=tuple(R) + (self.CF,), W=W)
        self.act(out, out, AF.Exp, scale=-0.5, R=W, W=W)

    def gated_norm_T(self, O, nq, normw, gate_ap, gate_bufs, OT, blk0, cols, post_scale=None):
        G = self.G
        SQ, YB, ST = G[10], G[11], self.SSM
        o = O[0:nq, 0:4, :]
        sq = SQ[0:nq, 0:256].rearrange("p (h e) -> p h e", e=64)
        self.tt(sq, o, o, ALU.mult, R=(O,), W=(SQ,))
        self.red(ST[0:nq, 0:4], sq, R=(SQ,), W=(ST,))
        self.rsqrt_act(ST[0:nq, 0:4], ST[0:nq, 0:4], 1.0 / 64.0, R=(ST,), W=(ST,))
        self.bmul(sq, o, ST[0:nq, 0:4], 4, R=(O, ST), W=(SQ,))
        nw = normw.rearrange("p (h e) -> p h e", e=64)
        if post_scale is None:
            self.tt(sq, sq, nw, ALU.mult, R=(SQ, self.SM), W=(SQ,))
        else:
            self.stt(sq, sq, post_scale, nw, ALU.mult, ALU.mult, R=(SQ, self.SM), W=(SQ,))
        yb = YB[0:nq, 256:384].bitcast(BF16)
        if gate_ap is not None:
            GT = G[12]
            gt = GT[0:nq, 0:256]
            self.act(gt, gate_ap, AF.Silu, R=gate_bufs, W=(GT,))
            self.tt(yb, SQ[0:nq, 0:256], gt, ALU.mult, R=(SQ, GT), W=(YB,))
        else:
            self.cp(yb, SQ[0:nq, 0:256], R=(SQ,), W=(YB,))
        pv = self.psv(0, [2, 128], BF16)
        for b in range(2):
            self.tr(pv[:, b, 0:nq], yb[:, b * 128:(b + 1) * 128], self.cbf("identb", parts=nq)[:, 0:nq], R=(YB, self.CB), W=(self.PS[0],))
        self.cp(OT[:, blk0:blk0 + 2, cols], pv[:, :, 0:nq], R=(self.PS[0],), W=(OT,), eng="act")


    def phase_m(self, l, first_phase):
        cfg = self.cfg
        ntp, nts, PAST = cfg["NTP"], cfg["NTS"], cfg["PAST"]
        NKP = PAST // 128
        mix = cfg.get("mix", "ABCD")
        mark = self.top
        PS = self.PS
        XT, hT = self.XT, self.hT
        Win = self.alloc("Win", [8, DIN], BF16)
        self.wload(Win, self.din["w_in"][l], 8)
        WO = [self.alloc("WO%d" % i, [1024], BF16) for i in range(2)]
        KTB = self.alloc("KTB", [2, ntp * 128], BF16)
        KTC = self.alloc("KTC", [2, ntp * 128], BF16)
        VB = self.alloc("VB", [ntp, 4, 65], BF16)
        VC = self.alloc("VC", [ntp, 256], BF16)
        CV = self.alloc("CV", [6, 134])
        U = self.alloc("U", [6, 128])
        QZ = self.alloc("QZ", [2, 4, 128], BF16)
        CQ = self.alloc("CQ", [2, 2, 128], BF16)
        DQ = self.alloc("DQ", [2, 128])
        DF = self.alloc("DF", [2, 128])
        STB = self.alloc("STB", [512])
        STC = STB
        DIG = self.alloc("DIG", [1, 512])
        AZ = self.alloc("AZ", [1, 264])
        OT = self.alloc("OT", [8, 128], BF16)
        SA = self.alloc("SA", [4, 64])
        SD = self.alloc("SD", [4, 64])
        self.SSM = self.alloc("SSM", [16])
        SM1 = self.alloc("SM1", [64])
        self.SM1b = SM1
        KTBs = self.alloc("KTBs", [2, 128], BF16)
        KTCs = self.alloc("KTCs", [2, 128], BF16)
        VBs = self.alloc("VBs", [2, 4, 65], BF16)
        VCs = self.alloc("VCs", [2, 256], BF16)
        stg = []
        for i in range(2):
            kb_ = self.alloc("skb%d" % i, [256], BF16)
            stg.append(dict(kb=kb_, vb=self.alloc("svb%d" % i, [4, 65], BF16),
                            kc=kb_, vc=self.alloc("svc%d" % i, [256], BF16),
                            kt=self.alloc("skt%d" % i, [2, 128], BF16)))
        self.G = [self.alloc("G%d" % i, [512]) for i in range(9)]
        self.G.append(self.alloc("G9", [16]))
        self.G.append(self.alloc("G10", [512]))
        g12 = self.alloc("G12", [512])
        self.G += [g12, g12, self.alloc("G13", [512])]
        G = self.G
        cf, sm = self.c, self.sm
        self.memset(QZ[:, :, :, :], 0.0, W=(QZ,))
        self.memset(CQ[:, :, :, :], 0.0, W=(CQ,))
        self.memset(VB[:, :, :, 64:65], 1.0, W=(VB,))
        self.memset(VBs[:, :, :, 64:65], 1.0, W=(VBs,))
        for i in range(2):
            self.memset(stg[i]["vb"][:, :, 64:65], 1.0, W=(stg[i]["vb"],))
        self.memset(SA[:, :, :], 0.0, W=(SA,))
        self.memset(SD[:, :, :], 0.0, W=(SD,))
        self.memset(CV[:, :, 0:3], 0.0, W=(CV,))
        self.act(SM1[:, 0:4], sm("alog")[:, l * 4:(l + 1) * 4], AF.Exp, R=(self.SM,), W=(SM1,))
        self.ts(SM1[:, 0:4], SM1[:, 0:4], -1.0, None, ALU.mult, R=(SM1,), W=(SM1,))
        if l == 0:
            self.memset(SM1[:, 4:6], 0.0, W=(SM1,), eng="dve")
        else:
            dl = sm("dlb")
            self.tt(SM1[:, 4:6], dl[:, 0:2], dl[:, 2:4], ALU.subtract, R=(self.SM,), W=(SM1,))
            self.act(SM1[:, 4:6], SM1[:, 4:6], AF.Exp, R=(SM1,), W=(SM1,))
            self.ts(SM1[:, 4:6], SM1[:, 4:6], 1.0, None, ALU.add, R=(SM1,), W=(SM1,))
            self.R.op("dve", lambda e: e.reciprocal(out=SM1[:, 4:6], in_=SM1[:, 4:6]), (SM1,), (SM1,))
        lv = sm("lamv")[:, l * 128:(l + 1) * 128].rearrange("p (a b) -> p a b", b=32)
        lt = G[0][:, 0:64].rearrange("p (a b) -> p a b", b=32)
        self.tt(lt[:, 0, :], lv[:, 0, :], lv[:, 1, :], ALU.mult, R=(self.SM,), W=(G[0],))
        self.tt(lt[:, 1, :], lv[:, 2, :], lv[:, 3, :], ALU.mult, R=(self.SM,), W=(G[0],))
        self.red(SM1[:, 10:12], lt, R=(G[0],), W=(SM1,))
        self.act(SM1[:, 10:12], SM1[:, 10:12], AF.Exp, R=(SM1,), W=(SM1,))
        lam_init = 0.8 - 0.6 * math.exp(-0.3 * l)
        self.tt(SM1[:, 8:9], SM1[:, 10:11], SM1[:, 11:12], ALU.subtract, R=(SM1,), W=(SM1,))
        self.ts(SM1[:, 9:10], SM1[:, 8:9], lam_init, -1.0, ALU.add, ALU.mult, R=(SM1,), W=(SM1,))
        NEGA, LB, NLAM = SM1[:, 0:4], SM1[:, 4:6], SM1[:, 9:10]
        identf, identb = cf("ident"), self.cbf("identb")

        def ldx(t):
            src = self.xin_rows(t) if first_phase else self.xrows(t)
            self.load(XT[t % 2], XT[t % 2][:, :], src)

        def wo_load(slot, k):
            self.load(WO[slot], WO[slot][:, :], self.din["w_out"][l, k * 128:(k + 1) * 128, :], q="pool")

        ev = [0]

        def evac(out, in_, R, W, scale=None):
            ev[0] += 1
            if ev[0] % 2:
                self.act(out, in_, AF.Copy, scale=(1.0 if scale is None else scale), R=R, W=W)
            elif scale is None:
                self.cp(out, in_, R=R, W=W)
            else:
                self.ts(out, in_, scale, None, ALU.mult, R=R, W=W)

        pb = [0]

        def fm_block(c0, rhs_cols=slice(0, 128)):
            pb[0] += 1
            b = 1 + pb[0] % 2
            n = rhs_cols.stop - rhs_cols.start
            pv = self.psv(b, [n])
            for k in range(8):
                self.mm(pv, Win[:, k, c0:c0 + 128], hT[:, k, rhs_cols], k == 0, k == 7, R=(Win, hT), W=(PS[b],))
            return pv, PS[b]

        def tm_group(c0, n, tok=slice(0, 128)):
            pb[0] += 1
            b = 1 + pb[0] % 2
            m = tok.stop - tok.start
            pv = self.psv(b, [n], parts=m)
            for k in range(8):
                self.mm(pv, hT[:, k, tok], Win[:, k, c0:c0 + n], k == 0, k == 7, R=(Win, hT), W=(PS[b],))
            return pv, PS[b]

        ldx(0)
        for t in range(ntp + nts):
            is_s = t >= ntp
            xb = XT[t % 2]
            if t + 1 < ntp + nts:
                ldx(t + 1)
            for k in range(2):
                wo_load(k, k)
            self.norm_T(xb[:, :], xb, l, hT)
            tcols = slice(t * 128, (t + 1) * 128)
            if is_s:
                for s in range(2):
                    sq = (t - ntp) * 2 + s
                    for bq in range(6):
                        self.load(CV, CV[:, bq, s * 67:s * 67 + 3],
                                  self.din["state_a_conv"][l, sq][:, bq * 128:(bq + 1) * 128].rearrange("t p -> p t"), q="sp", slow=True)
            for i in range(6):
                pv, pbuf = fm_block(i * 128)
                if is_s:
                    evac(CV[:, i, 3:67], pv[:, 0:64], (pbuf,), (CV,))
                    evac(CV[:, i, 70:134], pv[:, 64:128], (pbuf,), (CV,))
                else:
                    evac(CV[:, i, 3:131], pv, (pbuf,), (CV,))
            for p in range(2):
                pv, pbuf = fm_block(1032 + p * 128)
                for (slot, r0) in ((0, 0), (1, 32), (2, 64), (3, 96)):
                    evac(QZ[r0:r0 + 32, p, slot, :], pv[r0:r0 + 32, :], (pbuf,), (QZ,), scale=32 ** -0.5)
                pv, pbuf = fm_block(1288 + p * 128)
                evac(KTBs[:, p, :] if is_s else KTB[:, p, tcols], pv, (pbuf,), (KTBs if is_s else KTB,))
                pv, pbuf = fm_block(1800 + p * 128)
                for hl in range(2):
                    evac(CQ[hl * 64:(hl + 1) * 64, p, hl, :], pv[hl * 64:(hl + 1) * 64, :], (pbuf,), (CQ,), scale=0.125)
                pv, pbuf = fm_block(2056 + p * 128)
                evac(KTCs[:, p, :] if is_s else KTC[:, p, tcols], pv, (pbuf,), (KTCs if is_s else KTC,))
                pv, pbuf = fm_block(2568 + p * 128)
                evac(DQ[:, p, :], pv, (pbuf,), (DQ,), scale=0.125)
                pv, pbuf = fm_block(2824 + p * 128)
                evac(DF[:, p, :], pv, (pbuf,), (DF,))
            for (c0, ST, Vres, Vsm, okn, ovn, is_b) in ((1288, STB, VB, VBs, "b_k", "b_v", True), (2056, STC, VC, VCs, "c_k", "c_v", False)):
                if is_s:
                    for sg in range(2):
                        sq = (t - ntp) * 2 + sg
                        pv, pbuf = tm_group(c0, 512, slice(sg * 64, (sg + 1) * 64))
                        self.act(ST[0:64, :], pv, AF.Copy, R=(pbuf,), W=(ST,))
                        if is_b:
                            self.cp(Vsm[0:64, sg, :, 0:64], ST[0:64, 256:512].rearrange("p (h e) -> p h e", e=64), R=(ST,), W=(Vsm,))
                        else:
                            self.cp(Vsm[0:64, sg, :], ST[0:64, 256:512], R=(ST,), W=(Vsm,))
                        self.store(ST, self.dout["s_" + okn][l, sq, :, :], ST[0:64, 0:256])
                        self.store(ST, self.dout["s_" + ovn][l, sq, :, :], ST[0:64, 256:512])
                else:
                    pv, pbuf = tm_group(c0, 512)
                    self.act(ST[:, :], pv, AF.Copy, R=(pbuf,), W=(ST,))
                    if is_b:
                        self.cp(Vres[:, t, :, 0:64], ST[:, 256:512].rearrange("p (h e) -> p h e", e=64), R=(ST,), W=(Vres,))
                    else:
                        self.cp(Vres[:, t, :], ST[:, 256:512], R=(ST,), W=(Vres,))
                    self.store(ST, self.dout["p_" + okn][l, tcols, :], ST[:, 0:256])
                    self.store(ST, self.dout["p_" + ovn][l, tcols, :], ST[:, 256:512])
            if is_s or t == ntp - 1:
                for (c0, n) in ((0, 512), (512, 256)):
                    pv, pbuf = tm_group(c0, n)
                    self.act(G[0][:, 0:n], pv, AF.Copy, R=(pbuf,), W=(G[0],))
                    if is_s:
                        for s in range(2):
                            sq = (t - ntp) * 2 + s
                            self.store(G[0], self.dout["s_a_conv"][l, sq, :, c0:c0 + n], G[0][s * 64 + 61:s * 64 + 64, 0:n])
                    else:
                        self.store(G[0], self.dout["p_a_conv"][l, :, c0:c0 + n], G[0][125:128, 0:n])
            segs = [(0, 64, 0), (67, 64, 64)] if is_s else [(0, 128, 0)]
            cw = sm("convw")[:, l * 24:(l + 1) * 24].rearrange("p (b t) -> p b t", t=4)
            for i in range(6):
                for (o, n, oo) in segs:
                    dst = U[:, i, oo:oo + n]
                    self.ts(dst, CV[:, i, o + 3:o + 3 + n], cw[:, i, 3:4], None, ALU.mult, R=(CV, self.SM), W=(U,))
                    for tap in (2, 1, 0):
                        self.stt(dst, CV[:, i, o + tap:o + tap + n], cw[:, i, tap:tap + 1], dst, ALU.mult, ALU.add,
                                 R=(CV, self.SM, U), W=(U,))
                if not is_s:
                    self.cp(CV[:, i, 0:3], CV[:, i, 128:131], R=(CV,), W=(CV,), eng="pool")
            self.act(U[:, :, :], U[:, :, :], AF.Silu, R=(U,), W=(U,))
            for i in range(4):
                sqv = G[0][:, 0:128]
                self.tt(sqv, U[:, i, :], U[:, i, :], ALU.mult, R=(U,), W=(G[0],))
                pv = self.psv(7, [128])
                self.mm(pv, cf("blkones"), sqv, R=(self.CF, G[0]), W=(PS[7],))
                rv = G[1][:, 0:128]
                self.rsqrt_act(rv, pv, 1.0, R=(PS[7],), W=(G[1],))
                if i < 2:
                    self.stt(U[:, i, :], U[:, i, :], 0.125, rv, ALU.mult, ALU.mult, R=(U, G[1]), W=(U,))
                else:
                    self.tt(U[:, i, :], U[:, i, :], rv, ALU.mult, R=(U, G[1]), W=(U,))
            for c in range(2):
                tok = slice(c * 64, (c + 1) * 64)
                pv, pbuf = tm_group(3080, 512, tok)
                evac(DIG[0:64, 0, :], pv, (pbuf,), (DIG,))
                pv, pbuf = tm_group(768, 264, tok)
                evac(AZ[0:64, 0, :], pv, (pbuf,), (AZ,))
                if "A" in mix:
                    self.mix_A(l, t, c, U, AZ, SA, OT, NEGA)
                else:
                    self.memset(OT[:, 0:2, c * 64:(c + 1) * 64], 0.0, W=(OT,))
                if "D" in mix:
                    self.mix_D(l, t, c, DQ, DF, DIG, SD, OT, LB)
                else:
                    self.memset(OT[:, 6:8, c * 64:(c + 1) * 64], 0.0, W=(OT,))
            if "B" in mix:
                self.mix_B(l, t, QZ, KTB, VB, KTBs, VBs, stg, OT, NLAM, lam_init)
            else:
                self.memset(OT[:, 2:4, :], 0.0, W=(OT,))
            if "C" in mix:
                self.mix_C(l, t, CQ, KTC, VC, KTCs, VCs, stg, OT)
            else:
                self.memset(OT[:, 4:6, :], 0.0, W=(OT,))
            if "T" in mix:
                YBt = G[11]
                ybt = YBt[0:128, 256:384].bitcast(BF16)
                self.memset(ybt, 0.5, W=(YBt,))
                self.to_OT(YBt, ybt, 128, OT, 4, slice(0, 128))
            if "U" in mix:
                YBt = G[11]
                ybt = YBt[0:64, 256:384].bitcast(BF16)
                self.memset(ybt, 0.5, W=(YBt,))
                self.to_OT(YBt, ybt, 64, OT, 4, slice(0, 64))
            pv0, pv1 = self.psv(1, [512]), self.psv(2, [512])
            for j in range(8):
                w = WO[j % 2]
                self.mm(pv0, OT[:, j, :], w[:, 0:512], j == 0, j == 7, R=(OT, w), W=(PS[1],))
                self.mm(pv1, OT[:, j, :], w[:, 512:1024], j == 0, j == 7, R=(OT, w), W=(PS[2],))
                if j < 6:
                    wo_load(j % 2, j + 2)
            self.tt(xb[:, 0:512], xb[:, 0:512], pv0, ALU.add, R=(PS[1], xb), W=(xb,))
            self.tt(xb[:, 512:1024], xb[:, 512:1024], pv1, ALU.add, R=(PS[2], xb), W=(xb,))
            self.store(xb, self.xrows(t), xb[:, :])
        self.R.barrier()
        self.top = mark


    def state_io(self, l, t, c, S, in_name, out_p, out_s, load):
        ntp = self.cfg["NTP"]
        if t >= ntp:
            sq = (t - ntp) * 2 + c
            for h in range(4):
                hb = (h % 2) * 64
                if load:
                    self.load(S, S[hb:hb + 64, h, :], self.din[in_name][l, sq, h])
                else:
                    self.store(S, self.dout[out_s][l, sq, h], S[hb:hb + 64, h, :])
        elif (not load) and t == ntp - 1 and c == 1:
            for h in range(4):
                hb = (h % 2) * 64
                self.store(S, self.dout[out_p][l, h], S[hb:hb + 64, h, :])

    def mix_D(self, l, t, c, DQ, DF, DIG, SD, OT, LB):
        G, PS, cf = self.G, self.PS, self.c
        cols = slice(c * 64, (c + 1) * 64)
        self.state_io(l, t, c, SD, "state_d_S", "p_d_S", "s_d_S", True)
        if "d0" in self.cfg.get("dbg", ""):
            self.memset(OT[:, 6:8, cols], 0.0, W=(OT,))
            return
        KD, QB, KDEC = (G[i][:, 0:128].rearrange("p (a b) -> p a b", b=64) for i in (1, 2, 3))
        QD = G[0][:, 0:256].rearrange("p (a b) -> p a b", b=64)
        self.memset(QD, 0.0, W=(G[0],))
        T = [G[4][:, i * 64:(i + 1) * 64] for i in range(8)]
        EBL = G[5]
        ones = cf("ones")[:, 0:64]
        for p in range(2):
            fl, q = DF[:, p, cols], DQ[:, p, cols]
            e1, num, den, lf, bc, ta, tb = T[0], T[1], T[2], T[3], T[4], T[5], T[6]
            self.act(e1, fl, AF.Exp, scale=-1.0, R=(DF,), W=(G[4],))
            self.ts(num, e1, LB[:, p:p + 1], 1.0, ALU.mult, ALU.add, R=(G[4], self.SM1b), W=(G[4],))
            self.ts(den, e1, 1.0, None, ALU.add, R=(G[4],), W=(G[4],))
            self.R.op("dve", lambda e, den=den: e.reciprocal(out=den, in_=den), (G[4],), (G[4],))
            self.tt(num, num, den, ALU.mult, R=(G[4],), W=(G[4],))
            self.act(lf, num, AF.Ln, R=(G[4],), W=(G[4],))
            self.ts(num, num, -1.0, 1.0, ALU.mult, ALU.add, R=(G[4],), W=(G[4],))
            self.R.op("dve", lambda e, bc=bc, lf=lf: e.tensor_tensor_scan(out=bc, data0=ones, data1=lf, initial=0.0,
                                                                          op0=ALU.mult, op1=ALU.add),
                      (G[4], self.CF), (G[4],))
            mid, bl = bc[:, 31:32], bc[:, 63:64]
            self.ts(ta, bc, mid, 80.0, ALU.subtract, ALU.min, R=(G[4],), W=(G[4],))
            self.act(ta, ta, AF.Exp, R=(G[4],), W=(G[4],))
            for hl in range(2):
                hs = slice(hl * 64, (hl + 1) * 64)
                self.tt(QD[hs, 2 * p + hl, :], q[hs, :], ta[hs, :], ALU.mult, R=(DQ, G[4]), W=(G[0],))
            self.ts(tb, bc, -1.0, mid, ALU.mult, ALU.add, R=(G[4],), W=(G[4],))
            self.ts(tb, tb, 80.0, None, ALU.min, R=(G[4],), W=(G[4],))
            self.act(tb, tb, AF.Exp, R=(G[4],), W=(G[4],))
            self.tt(KD[:, p, :], num, tb, ALU.mult, R=(G[4],), W=(G[1],))
            self.act(ta, bc, AF.Exp, R=(G[4],), W=(G[4],))
            self.tt(QB[:, p, :], q, ta, ALU.mult, R=(DQ, G[4]), W=(G[2],))
            self.act(tb, bc, AF.Exp, scale=-1.0, bias=bl, R=(G[4],), W=(G[4],))
            self.tt(KDEC[:, p, :], num, tb, ALU.mult, R=(G[4],), W=(G[3],))
            self.act(EBL[:, p:p + 1], bl, AF.Exp, R=(G[4],), W=(G[5],))
        if "d1" in self.cfg.get("dbg", ""):
            self.memset(OT[:, 6:8, cols], 0.0, W=(OT,))
            return
        pa = self.psv(7, [4, 64], parts=64)
        for h in range(4):
            p, hb = h // 2, (h % 2) * 64
            self.mm(pa[:, h, :], KD[:, p, :], QD[:, h, :], R=(G[0], G[1]), W=(PS[7],))
        ATD = G[6][0:64, 0:256].rearrange("p (h e) -> p h e", e=64)
        self.tt(ATD, pa, cf("mask_ui", 64).rearrange("p (h e) -> p h e", e=64), ALU.mult, R=(PS[7], self.CF), W=(G[6],))
        if "d2" in self.cfg.get("dbg", ""):
            self.memset(OT[:, 6:8, cols], 0.0, W=(OT,))
            return
        pk = self.psv(3, [2, 128], parts=64)
        for p in range(2):
            self.tr(pk[:, p, :], KDEC[:, p, :], cf("ident"), R=(G[3], self.CF), W=(PS[3],))
        KDT = G[7][0:64, 0:256].rearrange("p (h e) -> p h e", e=64)
        self.act(KDT, pk.rearrange("p a (b e) -> p (a b) e", e=64), AF.Copy, R=(PS[3],), W=(G[7],))
        if "d3" in self.cfg.get("dbg", ""):
            self.memset(OT[:, 6:8, cols], 0.0, W=(OT,))
            return
        po = self.psv(4, [4, 64], parts=64)
        for h in range(4):
            p, hb = h // 2, (h % 2) * 64
            self.mm(po[:, h, :], ATD[:, h, :], DIG[0:64, 0, h * 64:(h + 1) * 64], True, False, R=(G[6], DIG), W=(PS[4],))
            self.mm(po[:, h, :], QB[:, p, :], SD[:, h, :], False, True, R=(G[2], SD), W=(PS[4],))
        pn = self.psv(5, [2, 64])
        for h in range(4):
            p, hb = h // 2, (h % 2) * 64
            self.mm(pn[hb:hb + 64, p, :], KDT[:, h, :], DIG[0:64, 0, h * 64:(h + 1) * 64], R=(G[7], DIG), W=(PS[5],))
        if "d4" in self.cfg.get("dbg", ""):
            self.memset(OT[:, 6:8, cols], 0.0, W=(OT,))
            return
        for h in range(4):
            p, hs = h // 2, slice((h % 2) * 64, (h % 2) * 64 + 64)
            self.stt(SD[hs, h, :], SD[hs, h, :], EBL[hs, p:p + 1], pn[hs, p, :], ALU.mult, ALU.add, R=(SD, G[5], PS[5]), W=(SD,))
        O = G[8]
        self.act(O[0:64, 0:256].rearrange("p (h e) -> p h e", e=64), po, AF.Copy, R=(PS[4],), W=(G[8],))
        if "d5" in self.cfg.get("dbg", ""):
            self.memset(OT[:, 6:8, cols], 0.0, W=(OT,))
            return
        Ov = Buf(O[0:64, 0:256].rearrange("p (h e) -> p h e", e=64), "Ov")
        self.gated_norm_T2(G[8], 64, self.sm("dnorm")[0:64, l * 256:(l + 1) * 256], DIG[0:64, 0, 256:512], (DIG,), OT, 6, cols)
        self.state_io(l, t, c, SD, "state_d_S", "p_d_S", "s_d_S", False)

    def gated_norm_T2(self, Obuf, nq, normw, gate_ap, gate_bufs, OT, blk0, cols, post_scale=None):
        G = self.G
        SQ, YB, ST = G[10], G[11], self.SSM
        o = Obuf[0:nq, 0:256].rearrange("p (h e) -> p h e", e=64)
        sq = SQ[0:nq, 0:256].rearrange("p (h e) -> p h e", e=64)
        self.tt(sq, o, o, ALU.mult, R=(Obuf,), W=(SQ,))
        self.red(ST[0:nq, 0:4], sq, R=(SQ,), W=(ST,))
        self.rsqrt_act(ST[0:nq, 0:4], ST[0:nq, 0:4], 1.0 / 64.0, R=(ST,), W=(ST,))
        self.bmul(sq, o, ST[0:nq, 0:4], 4, R=(Obuf, ST), W=(SQ,))
        nw = normw.rearrange("p (h e) -> p h e", e=64)
        if post_scale is None:
            self.tt(sq, sq, nw, ALU.mult, R=(SQ, self.SM), W=(SQ,))
        else:
            self.stt(sq, sq, post_scale, nw, ALU.mult, ALU.mult, R=(SQ, self.SM), W=(SQ,))
        yb = YB[0:nq, 256:384].bitcast(BF16)
        if gate_ap is not None:
            GT = G[12]
            gt = GT[0:nq, 0:256]
            self.act(gt, gate_ap, AF.Silu, R=gate_bufs, W=(GT,))
            self.tt(yb, SQ[0:nq, 0:256], gt, ALU.mult, R=(SQ, GT), W=(YB,))
        else:
            self.cp(yb, SQ[0:nq, 0:256], R=(SQ,), W=(YB,))
        self.to_OT(YB, yb, nq, OT, blk0, cols)

    def to_OT(self, YB, yb, nq, OT, blk0, cols):
        pv = self.psv(0, [2, 128], BF16)
        for b in range(2):
            self.tr(pv[:, b, 0:nq], yb[:, b * 128:(b + 1) * 128], self.cbf("identb", parts=nq)[:, 0:nq], R=(YB, self.CB), W=(self.PS[0],))
        self.cp(OT[:, blk0:blk0 + 2, cols], pv[:, :, 0:nq], R=(self.PS[0],), W=(OT,), eng="act")

    def mix_A(self, l, t, c, U, AZ, SA, OT, NEGA):
        G, PS, cf, sm = self.G, self.PS, self.c, self.sm
        cols = slice(c * 64, (c + 1) * 64)
        self.state_io(l, t, c, SA, "state_a_S", "p_a_S", "s_a_S", True)
        v4 = lambda b: b[0:64, 0:256].rearrange("p (h e) -> p h e", e=64)
        S = self.SSM
        az = AZ[0:64, 0, :]
        BETA, NB, GS, GC, EG = S[0:64, 0:4], S[0:64, 4:8], S[0:64, 8:12], S[0:64, 12:16], G[9][0:64, 0:4]
        EGL = G[9][:, 8:12]
        self.act(BETA, az[:, 256:260], AF.Exp, scale=-1.0, R=(AZ,), W=(S,))
        self.ts(BETA, BETA, 1.0, None, ALU.add, R=(S,), W=(S,))
        self.R.op("dve", lambda e: e.reciprocal(out=BETA, in_=BETA), (S,), (S,))
        self.ts(NB, BETA, -1.0, None, ALU.mult, R=(S,), W=(S,))
        self.tt(GS, az[:, 260:264], sm("dtb", 64)[:, l * 4:(l + 1) * 4], ALU.add, R=(AZ, self.SM), W=(S,))
        self.act(GS, GS, AF.Exp, R=(S,), W=(S,))
        self.act(GS, GS, AF.Ln, bias=cf("ones", 64)[:, 0:1], R=(S, self.CF), W=(S,))
        self.tt(GS, GS, NEGA[0:64, :], ALU.mult, R=(S, self.SM1b), W=(S,))
        tri = cf("tri64", 64)
        pg = self.psv(7, [4], parts=64)
        self.mm(pg, tri, GS, R=(self.CF, S), W=(PS[7],))
        self.cp(GC, pg, R=(PS[7],), W=(S,))
        GB = G[0][0:64, 0:512].rearrange("p (h e) -> p h e", e=128)
        for h in range(4):
            self.ts(GB[:, h, :], cf("ones", 64), GS[:, h:h + 1], None, ALU.mult, R=(self.CF, S), W=(G[0],))
        pw = self.psv(6, [4, 64])
        for h in range(4):
            self.mm(pw[:, h, :], GB[:, h, :], tri, R=(G[0], self.CF), W=(PS[6],))
        GW = G[1][:, 0:256].rearrange("p (h e) -> p h e", e=64)
        self.act(GW, pw, AF.Copy, R=(PS[6],), W=(G[1],))
        self.act(EG, GC, AF.Exp, R=(S,), W=(G[9],))
        self.act(EGL, GW[:, :, 63], AF.Exp, R=(G[1],), W=(G[9],))
        T1, T2 = v4(G[2]), v4(G[3])
        mneg, mpos = v4(Buf(cf("mneg_ui"), "x")), v4(Buf(cf("mpos_sl"), "x"))
        for h in range(4):
            self.stt(T1[:, h, :], GW[0:64, h, :], GC[:, h:h + 1], mneg[:, h, :], ALU.subtract, ALU.min, R=(G[1], S, self.CF), W=(G[2],))
            self.stt(T2[:, h, :], GW[0:64, h, :], GC[:, h:h + 1], mpos[:, h, :], ALU.subtract, ALU.max, R=(G[1], S, self.CF), W=(G[3],))
        self.act(T1, T1, AF.Exp, R=(G[2],), W=(G[2],))
        self.act(T2, T2, AF.Exp, scale=-1.0, R=(G[3],), W=(G[3],))
        pkk, pkq = self.psv(3, [4, 64], parts=64), self.psv(4, [4, 64], parts=64)
        KZ = G[8][:, 0:512].rearrange("p (a h e) -> p a h e", a=2, e=64)
        self.memset(KZ, 0.0, W=(G[8],))
        for h in range(4):
            p, hs = h // 2, slice((h % 2) * 64, (h % 2) * 64 + 64)
            self.cp(KZ[hs, 0, h, :], U[hs, 2 + p, cols], R=(U,), W=(G[8],), eng="pool")
            self.cp(KZ[hs, 1, h, :], U[hs, p, cols], R=(U,), W=(G[8],), eng="pool")
        for h in range(4):
            p = h // 2
            self.mm(pkk[:, h, :], U[:, 2 + p, cols], KZ[:, 0, h, :], R=(U, G[8]), W=(PS[3],))
            self.mm(pkq[:, h, :], U[:, 2 + p, cols], KZ[:, 1, h, :], R=(U, G[8]), W=(PS[4],))
        XM, AQK = v4(G[4]), v4(G[5])
        for h in range(4):
            self.stt(XM[:, h, :], pkk[:, h, :], NB[:, h:h + 1], T2[:, h, :], ALU.mult, ALU.mult, R=(PS[3], S, G[3]), W=(G[4],))
        self.tt(AQK, pkq, T1, ALU.mult, R=(PS[4], G[2]), W=(G[5],))
        i64 = cf("ident", 64)[:, 0:64]
        pn_ = self.psv(5, [4, 64], parts=64)
        for h in range(4):
            self.tr(pn_[:, h, :], XM[:, h, :], i64, R=(G[4], self.CF), W=(PS[5],))
        NM = v4(G[6])
        self.act(NM, pn_, AF.Copy, R=(PS[5],), W=(G[6],))
        RM = v4(G[7])
        i4 = cf("ident64", 64)
        for h in range(4):
            self.tt(RM[:, h, :], NM[:, h, :], i4, ALU.add, R=(G[6], self.CF), W=(G[7],))
        Pb, Qb = [G[6], G[0]], [G[4], G[3]]
        cur = 0
        for k in range(5):
            P, Q = v4(Pb[cur]), v4(Qb[cur])
            Pn, Qn = v4(Pb[1 - cur]), v4(Qb[1 - cur])
            pp = self.psv(3, [8, 64], parts=64)
            for h in range(4):
                self.mm(pp[:, h, :], Q[:, h, :], P[:, h, :], R=(Pb[cur], Qb[cur]), W=(PS[3],))
                self.mm(pp[:, 4 + h, :], P[:, h, :], Q[:, h, :], R=(Pb[cur], Qb[cur]), W=(PS[3],))
            self.act(Pn, pp[:, 0:4, :], AF.Copy, R=(PS[3],), W=(Pb[1 - cur],))
            self.cp(Qn, pp[:, 4:8, :], R=(PS[3],), W=(Qb[1 - cur],))
            pr = self.psv(4, [4, 64], parts=64)
            for h in range(4):
                self.mm(pr[:, h, :], Qn[:, h, :], RM[:, h, :], R=(Qb[1 - cur], G[7]), W=(PS[4],))
            self.tt(RM, RM, pr, ALU.add, R=(G[7], PS[4]), W=(G[7],))
            cur = 1 - cur
        pt = self.psv(5, [4, 128], parts=64)
        for p in range(2):
            self.tr(pt[:, p, :], U[:, 4 + p, cols], cf("ident"), R=(U, self.CF), W=(PS[5],))
            self.tr(pt[:, 2 + p, :], U[:, 2 + p, cols], cf("ident"), R=(U, self.CF), W=(PS[5],))
        VT, KTt = v4(G[0]), v4(G[3])
        self.act(VT, pt[:, 0:2, :].rearrange("p a (b e) -> p (a b) e", e=64), AF.Copy, R=(PS[5],), W=(G[0],))
        self.cp(KTt, pt[:, 2:4, :].rearrange("p a (b e) -> p (a b) e", e=64), R=(PS[5],), W=(G[3],))
        pm1, pqs = self.psv(3, [4, 64], parts=64), self.psv(6, [4, 64], parts=64)
        for h in range(4):
            p, hb = h // 2, (h % 2) * 64
            self.mm(pm1[:, h, :], U[:, 2 + p, cols], SA[:, h, :], R=(U, SA), W=(PS[3],))
            self.mm(pqs[:, h, :], U[:, p, cols], SA[:, h, :], R=(U, SA), W=(PS[6],))
        RW = v4(G[4])
        self.bmul(RW, pm1, EG, 4, R=(PS[3], G[9]), W=(G[4],))
        self.tt(RW, VT, RW, ALU.subtract, R=(G[0], G[4]), W=(G[4],))
        self.bmul(RW, RW, BETA, 4, R=(G[4], S), W=(G[4],))
        pw_ = self.psv(4, [4, 64], parts=64)
        for h in range(4):
            self.mm(pw_[:, h, :], RM[:, h, :], RW[:, h, :], R=(G[7], G[4]), W=(PS[4],))
        WS = v4(G[6])
        self.act(WS, pw_, AF.Copy, R=(PS[4],), W=(G[6],))
        QSG = v4(G[13])
        self.bmul(QSG, pqs, EG, 4, R=(PS[6], G[9]), W=(G[13],))
        po2 = self.psv(3, [4, 64], parts=64)
        for h in range(4):
            self.mm(po2[:, h, :], AQK[:, h, :], WS[:, h, :], R=(G[5], G[6]), W=(PS[3],))
        self.tt(QSG, QSG, po2, ALU.add, R=(G[13], PS[3]), W=(G[13],))
        KDA = v4(G[0])
        for h in range(4):
            self.ts(KDA[:, h, :], KTt[:, h, :], T1[:, h, 63:64], None, ALU.mult, R=(G[3], G[2]), W=(G[0],))
        psn = self.psv(5, [2, 64])
        for h in range(4):
            p, hb = h // 2, (h % 2) * 64
            self.mm(psn[hb:hb + 64, p, :], KDA[:, h, :], WS[:, h, :], R=(G[0], G[6]), W=(PS[5],))
        for h in range(4):
            p, hb = h // 2, (h % 2) * 64
            self.stt(SA[hb:hb + 64, h, :], SA[hb:hb + 64, h, :], EGL[hb:hb + 64, h:h + 1], psn[hb:hb + 64, p, :], ALU.mult, ALU.add,
                     R=(SA, G[9], PS[5]), W=(SA,))
        self.gated_norm_T2(G[13], 64, sm("anorm")[0:64, l * 256:(l + 1) * 256], az[:, 0:256], (AZ,), OT, 0, cols)
        self.state_io(l, t, c, SA, "state_a_S", "p_a_S", "s_a_S", False)


    def _qblocks(self, t):
        ntp = self.cfg["NTP"]
        if t < ntp:
            return [(slice(0, 128), 128, 0, None)]
        return [(slice(s * 64, (s + 1) * 64), 64, s * 64, (t - ntp) * 2 + s) for s in range(2)]

    def _stage(self, l, sq, kt, st, kname, vname, is_b):
        PS = self.PS
        kb, vb = (st["kb"], st["vb"]) if is_b else (st["kc"], st["vc"])
        rows = slice(kt * 128, (kt + 1) * 128)
        self.load(kb, kb[:, :], self.din[kname][l, sq, rows, :], q="pool")
        if is_b:
            self.load(vb, vb[:, :, 0:64], self.din[vname][l, sq, rows, :].rearrange("p (h e) -> p h e", e=64), q="pool")
        else:
            self.load(vb, vb[:, :], self.din[vname][l, sq, rows, :], q="pool")
        pv = self.psv(0, [2, 128], BF16)
        for p in range(2):
            self.tr(pv[:, p, :], kb[:, p * 128:(p + 1) * 128], self.cbf("identb"), R=(kb, self.CB), W=(PS[0],))
        self.cp(st["kt"][:, :, :], pv, R=(PS[0],), W=(st["kt"],), eng="act")
        return st["kt"], vb

    def mix_B(self, l, t, QZ, KTB, VB, KTBs, VBs, stg, OT, NLAM, lam_init):
        G, PS, cf, sm = self.G, self.PS, self.c, self.sm
        ntp, PAST = self.cfg["NTP"], self.cfg["PAST"]
        PT = G[0][:, 0:512].bitcast(BF16).rearrange("p (a b) -> p a b", b=128)
        SDg = [G[1], G[2]]
        bias = cf("bias")
        for (qc, nq, poff, sq) in self._qblocks(t):
            sps = [self.psv(3, [4, 128]), self.psv(4, [4, 128])]
            acc = [self.psv(5, [4, 65]), self.psv(6, [4, 65])]
            if sq is None:
                tiles = [("far", kt, t - kt) for kt in range(t)] + [("diag", t, 0)]
            else:
                tiles = [("cache", kt, PAST // 128 - kt) for kt in range(PAST // 128)] + [("own", 0, 0)]
            zb = self.cbf("zerob")
            for bk in range(2):
                self.mm(acc[bk][0:nq, :, :], zb[:, 0:nq], zb[:, 0:260], True, False, R=(self.CB,), W=(PS[5 + bk],))
            for ti, (kind, kt, n) in enumerate(tiles):
                first, last = ti == 0, ti == len(tiles) - 1
                if kind == "far":
                    KT, KTbuf, V, Vbuf = KTB[:, :, kt * 128:(kt + 1) * 128], KTB, VB[:, kt, :, :], VB
                    ks, npk = slice(0, 128), 128
                elif kind == "diag":
                    KT, KTbuf, V, Vbuf = KTB[:, :, t * 128:(t + 1) * 128], KTB, VB[:, t, :, :], VB
                    ks, npk = slice(0, 128), 128
                elif kind == "cache":
                    st = stg[kt % 2]
                    KTbuf, Vbuf = self._stage(l, sq, kt, st, "cache_b_k", "cache_b_v", True)
                    KT, V = KTbuf[:, :, :], Vbuf[:, :, :]
                    ks, npk = slice(0, 128), 128
                else:
                    KT, KTbuf, V, Vbuf = KTBs[:, :, qc], KTBs, VBs[:, poff // 64, :, :], VBs
                    ks, npk = slice(0, 64), 64
                for h in range(4):
                    p, hb = h // 2, (h % 2) * 64
                    for c in range(2):
                        i = h * 2 + c
                        self.mm(sps[i // 4][ks, i % 4, 0:nq], KT[:, p, :], QZ[:, p, (h % 2) * 2 + c, qc],
                                R=(KTbuf, QZ), W=(PS[3 + i // 4],))
                if kind in ("far", "cache"):
                    for h in range(4):
                        self.act(PT[:, h * 2:h * 2 + 2, 0:nq], sps[h // 2][:, (h % 2) * 2:(h % 2) * 2 + 2, 0:nq], AF.Exp,
                                 bias=bias[:, h * 33 + n:h * 33 + n + 1], R=(PS[3 + h // 2], self.CF), W=(G[0],))
                else:
                    for h in range(4):
                        if kind == "diag":
                            db = cf("diagb")[:, h * 128:(h + 1) * 128]
                        else:
                            db = cf("diags")[ks, h * 64:(h + 1) * 64]
                        for c in range(2):
                            i = h * 2 + c
                            sd = SDg[i // 4][ks, (i % 4) * 128:(i % 4) * 128 + nq]
                            self.tt(sd, sps[i // 4][ks, i % 4, 0:nq], db, ALU.add, R=(PS[3 + i // 4], self.CF), W=(SDg[i // 4],))
                    for half in range(2):
                        sdv = SDg[half][ks, 0:512].rearrange("p (a b) -> p a b", b=128)
                        self.act(PT[ks, half * 4:(half + 1) * 4, 0:nq], sdv[:, :, 0:nq], AF.Exp, R=(SDg[half],), W=(G[0],))
                for h in range(4):
                    for c in range(2):
                        i = h * 2 + c
                        self.mm(acc[i // 4][0:nq, i % 4, :], PT[ks, i, 0:nq], V[ks, h, :], False, last and (i % 4 == 3),
                                R=(G[0], Vbuf), W=(PS[5 + i // 4],))
            ACs = G[3][0:nq, 0:520].rearrange("p (a b) -> p a b", b=65) if False else None
            A0 = G[3][0:nq, 0:260].rearrange("p (a b) -> p a b", b=65)
            A1 = G[4][0:nq, 0:260].rearrange("p (a b) -> p a b", b=65)
            self.act(A0, acc[0][0:nq, :, :], AF.Copy, R=(PS[5],), W=(G[3],))
            self.act(A1, acc[1][0:nq, :, :], AF.Copy, R=(PS[6],), W=(G[4],))
            RD = self.SSM
            self.R.op("dve", lambda e, A0=A0, nq=nq: e.reciprocal(out=RD[0:nq, 0:4], in_=A0[:, :, 64]), (G[3],), (RD,))
            self.R.op("dve", lambda e, A1=A1, nq=nq: e.reciprocal(out=RD[0:nq, 4:8], in_=A1[:, :, 64]), (G[4],), (RD,))
            ON = G[5][0:nq, 0:512].rearrange("p (a b) -> p a b", b=64)
            for i in range(8):
                src = (A0 if i < 4 else A1)[:, i % 4, 0:64]
                self.ts(ON[:, i, :], src, RD[0:nq, i:i + 1], None, ALU.mult, R=(G[3], G[4], RD), W=(G[5],))
            ON4 = G[5][0:nq, 0:512].rearrange("p (h c e) -> p h c e", c=2, e=64)
            OB = G[8]
            obv = OB[0:nq, 0:256].rearrange("p (h e) -> p h e", e=64)
            self.stt(obv, ON4[:, :, 1, :], NLAM[0:nq, :], ON4[:, :, 0, :], ALU.mult, ALU.add, R=(G[5], self.SM1b), W=(G[8],))
            self.gated_norm_T2(OB, nq, sm("bnorm")[0:nq, l * 256:(l + 1) * 256], None, (), OT, 2, qc, post_scale=1.0 - lam_init)

    def mix_C(self, l, t, CQ, KTC, VC, KTCs, VCs, stg, OT):
        G, PS, cf = self.G, self.PS, self.c
        ntp, PAST = self.cfg["NTP"], self.cfg["PAST"]
        EZ, TT, EC, RR = G[0], G[1], G[2], G[3]
        SPb = G[4][:, 0:256].bitcast(BF16)
        ATb = G[5][:, 0:256].bitcast(BF16)
        utri, onesb = self.cbf("utri"), self.cbf("onesb")
        for (qc, nq, poff, sq) in self._qblocks(t):
            N4 = 4 * nq
            v3 = lambda ap: ap.rearrange("p (h q) -> p h q", q=nq)
            zp, cpv, rpv = self.psv(3, [4, nq]), self.psv(4, [N4]), self.psv(5, [N4])
            accc = self.psv(6, [4, 64])
            if sq is None:
                tiles = [("diag", t)] + [("far", kt) for kt in range(t - 1, -1, -1)]
            else:
                tiles = [("own", 0)] + [("cache", kt) for kt in range(PAST // 128 - 1, -1, -1)]
            zb = self.cbf("zerob")
            self.mm(accc[0:nq, :, :], zb[:, 0:nq], zb[:, 0:256], True, False, R=(self.CB,), W=(PS[6],))
            for ti, (kind, kt) in enumerate(tiles):
                first, last = ti == 0, ti == len(tiles) - 1
                if kind in ("far", "diag"):
                    KT, KTbuf, V, Vbuf = KTC[:, :, kt * 128:(kt + 1) * 128], KTC, VC[:, kt, :], VC
                    ks = slice(0, 128)
                elif kind == "cache":
                    st = stg[kt % 2]
                    KTbuf, Vbuf = self._stage(l, sq, kt, st, "cache_c_k", "cache_c_v", False)
                    KT, V = KTbuf[:, :, :], Vbuf[:, :]
                    ks = slice(0, 128)
                else:
                    KT, KTbuf, V, Vbuf = KTCs[:, :, qc], KTCs, VCs[:, poff // 64, :], VCs
                    ks = slice(0, 64)
                for h in range(4):
                    p, hb = h // 2, (h % 2) * 64
                    self.mm(zp[ks, h, :], KT[:, p, :], CQ[:, p, h % 2, qc], R=(KTbuf, CQ), W=(PS[3],))
                ez, sp = EZ[ks, 0:N4], SPb[ks, 0:N4]
                self.act(v3(ez), zp[ks, :, :], AF.Exp, R=(PS[3],), W=(EZ,))
                if first:
                    spf = TT[ks, 0:N4]
                    self.act(spf, ez, AF.Ln, bias=cf("ones")[ks, 0:1], R=(EZ, self.CF), W=(TT,))
                    cm = cf("cmask")[ks, ks.start:ks.start + nq]
                    for h in range(4):
                        self.tt(v3(sp)[:, h, :], v3(spf)[:, h, :], cm, ALU.mult, R=(TT, self.CF), W=(G[4],))
                        self.tt(v3(ez)[:, h, :], v3(ez)[:, h, :], cm, ALU.mult, R=(EZ, self.CF), W=(EZ,))
                else:
                    self.act(sp, ez, AF.Ln, bias=cf("ones")[ks, 0:1], R=(EZ, self.CF), W=(G[4],))
                self.mm(cpv[ks, :], utri[ks, ks], sp, R=(self.CB, G[4]), W=(PS[4],))
                self.mm(rpv, onesb[ks, :], sp, R=(self.CB, G[4]), W=(PS[5],))
                ec = EC[ks, 0:N4]
                if first:
                    self.act(ec, cpv[ks, :], AF.Exp, scale=-1.0, R=(PS[4],), W=(EC,))
                    self.cp(RR[:, 0:N4], rpv, R=(PS[5],), W=(RR,))
                else:
                    tt_ = TT[ks, 0:N4]
                    self.tt(tt_, cpv[ks, :], RR[ks, 0:N4], ALU.add, R=(PS[4], RR), W=(TT,))
                    self.act(ec, tt_, AF.Exp, scale=-1.0, R=(TT,), W=(EC,))
                    if not last:
                        self.tt(RR[:, 0:N4], RR[:, 0:N4], rpv, ALU.add, R=(RR, PS[5]), W=(RR,))
                at = ATb[ks, 0:N4]
                self.tt(at, ez, ec, ALU.mult, R=(EZ, EC), W=(G[5],))
                for h in range(4):
                    self.mm(accc[0:nq, h, :], v3(at)[:, h, :], V[ks, h * 64:(h + 1) * 64], False, last and h == 3, R=(G[5], Vbuf), W=(PS[6],))
            YB = G[11]
            yb = YB[0:nq, 256:384].bitcast(BF16)
            self.act(yb.rearrange("p (h e) -> p h e", e=64), accc[0:nq, :, :], AF.Copy, R=(PS[6],), W=(YB,))
            self.to_OT(YB, yb, nq, OT, 4, qc)

    def phase_x(self, l, first_phase):
        cfg = self.cfg
        ntp, nts = cfg["NTP"], cfg["NTS"]
        mark = self.top
        Wcq = self.alloc("Wcq", [8, 1024], BF16)
        Wco = self.alloc("Wco", [8, 1024], BF16)
        Wck = self.alloc("Wck", [8, 1024], BF16)
        Wcv = self.alloc("Wcv", [8, 1024], BF16)
        self.wload(Wck, self.din["w_ck"][l], 8)
        self.wload(Wcv, self.din["w_cv"][l], 8)
        self.wload(Wcq, self.din["w_cq"][l], 8)
        self.wload(Wco, self.din["w_co"][l], 8)
        MKT = self.alloc("MKT", [8, 256], BF16)
        MV = self.alloc("MV", [2, 1024], BF16)
        MHT = self.alloc("MHT", [8, 256], BF16)
        MST = self.alloc("MST", [2, 512])
        smem = []
        for s in range(2):
            smem.append((self.alloc("SKS%d" % s, [2, 1024], BF16), self.alloc("SMKT%d" % s, [8, 256], BF16),
                         self.alloc("SMV%d" % s, [2, 1024], BF16)))
        CQT = self.alloc("CQT", [8, 128], BF16)
        PT = self.alloc("PT", [8, 128], BF16)
        COT = self.alloc("COT", [8, 128], BF16)
        LN = self.alloc("LN", [4, 128])
        RC = self.alloc("RC", [4, 128])
        PS = self.PS
        XT, hT = self.XT, self.hT
        for mt in range(2):
            xb = XT[mt]
            self.load(xb, xb[:, :], self.din["memp"][mt * 128:(mt + 1) * 128, :])
            self.norm_T(xb[:, :], xb, 6 + l, MHT, tcols=slice(mt * 128, (mt + 1) * 128))
        for j in range(8):
            if "nomk" in cfg.get("dbg", ""):
                break
            pv = self.psv(1 + j % 2, [256])
            for k in range(8):
                self.mm(pv, Wck[:, k, j * 128:(j + 1) * 128], MHT[:, k, :], k == 0, k == 7, R=(Wck, MHT), W=(PS[1 + j % 2],))
            self.cp(MKT[:, j, :], pv, R=(PS[1 + j % 2],), W=(MKT,), eng="act" if j % 2 else "dve")
        si = 0
        for (Wm, is_v, oname) in ((Wck, False, "p_mem_k"), (Wcv, True, "p_mem_v")):
            if "nomv" in cfg.get("dbg", ""):
                break
            for mt in range(2):
                for hf in range(2):
                    b = 3 + si % 2
                    pv = self.psv(b, [512])
                    for k in range(8):
                        self.mm(pv, MHT[:, k, mt * 128:(mt + 1) * 128], Wm[:, k, hf * 512:(hf + 1) * 512], k == 0, k == 7,
                                R=(Wm, MHT), W=(PS[b],))
                    self.cp(MST[:, si % 2, :], pv, R=(PS[b],), W=(MST,), eng="act")
                    if is_v:
                        self.cp(MV[:, mt, hf * 512:(hf + 1) * 512], pv, R=(PS[b],), W=(MV,))
                    self.store(MST, self.dout[oname][l, mt * 128:(mt + 1) * 128, hf * 512:(hf + 1) * 512], MST[:, si % 2, :])
                    si += 1
        def ldx(t):
            src = self.xin_rows(t) if first_phase else self.xrows(t)
            self.load(XT[t % 2], XT[t % 2][:, :], src)
        ldx(0)
        for t in range(ntp + nts):
            if "notiles" in cfg.get("dbg", ""):
                break
            xb = XT[t % 2]
            if t + 1 < ntp + nts:
                ldx(t + 1)
            if t < ntp or "nosamp" in cfg.get("dbg", ""):
                segs = [(slice(0, 128), MKT, MV)]
            else:
                segs = []
                for s in range(2):
                    KS, SMKT, SMV = smem[s]
                    sq = (t - ntp) * 2 + s
                    self.load(KS, KS[:, :, :], self.din["cache_mem_k"][l, sq].rearrange("(a p) d -> p a d", p=128), q="pool")
                    self.load(SMV, SMV[:, :, :], self.din["cache_mem_v"][l, sq].rearrange("(a p) d -> p a d", p=128), q="pool")
                    for j in range(8):
                        b = 1 + j % 2
                        pv = self.psv(b, [256], BF16)
                        for mt in range(2):
                            self.tr(pv[:, mt * 128:(mt + 1) * 128], KS[:, mt, j * 128:(j + 1) * 128], self.cbf("identb"),
                                    R=(KS, self.CB), W=(PS[b],))
                        self.cp(SMKT[:, j, :], pv, R=(PS[b],), W=(SMKT,), eng="act" if j % 2 else "dve")
                    segs.append((slice(s * 64, (s + 1) * 64), SMKT, SMV))
            self.norm_T(xb[:, :], xb, 2 + l, hT)
            for half in range(2):
                b = 1 + half
                pv = self.psv(b, [4, 128])
                for jj in range(4):
                    j = half * 4 + jj
                    for k in range(8):
                        self.mm(pv[:, jj, :], Wcq[:, k, j * 128:(j + 1) * 128], hT[:, k, :], k == 0, k == 7, R=(Wcq, hT), W=(PS[b],))
                self.act(CQT[:, half * 4:(half + 1) * 4, :], pv, AF.Copy, scale=1.0 / 16.0, R=(PS[b],), W=(CQT,))
            for (cs, mkt, mv) in segs:
                for h in range(4):
                    b = 3 + h // 2
                    pv = self.psv(b, [4, 128])
                    for mt in range(2):
                        for c in range(2):
                            self.mm(pv[:, (h % 2) * 2 + mt, cs], mkt[:, h * 2 + c, mt * 128:(mt + 1) * 128], CQT[:, h * 2 + c, cs],
                                    c == 0, c == 1, R=(mkt, CQT), W=(PS[b],))
            for half in range(2):
                self.act(PT[:, half * 4:(half + 1) * 4, :], self.psv(3 + half, [4, 128]), AF.Exp, R=(PS[3 + half],), W=(PT,))
            for (cs, mkt, mv) in segs:
                for h in range(4):
                    b = 5 + h // 2
                    pv = self.psv(b, [4, 128])
                    for c in range(2):
                        for mt in range(2):
                            self.mm(pv[:, (h % 2) * 2 + c, cs], mv[:, mt, h * 256 + c * 128:h * 256 + (c + 1) * 128],
                                    PT[:, h * 2 + mt, cs], mt == 0, mt == 1, R=(mv, PT), W=(PS[b],))
                    pn = self.psv(7, [4, 128])
                    for mt in range(2):
                        self.mm(pn[:, h, cs], self.cbf("onesb"), PT[:, h * 2 + mt, cs], mt == 0, mt == 1, R=(self.CB, PT), W=(PS[7],))
            self.act(LN[:, :, :], self.psv(7, [4, 128]), AF.Ln, R=(PS[7],), W=(LN,))
            self.act(RC[:, :, :], LN[:, :, :], AF.Exp, scale=-1.0, R=(LN,), W=(RC,))
            for h in range(4):
                b = 5 + h // 2
                pv = self.psv(b, [4, 128])
                for c in range(2):
                    self.tt(COT[:, h * 2 + c, :], pv[:, (h % 2) * 2 + c, :], RC[:, h, :], ALU.mult, R=(PS[b], RC), W=(COT,))
            for hf in range(2):
                b = 1 + hf
                pv = self.psv(b, [512])
                for j in range(8):
                    self.mm(pv, COT[:, j, :], Wco[:, j, hf * 512:(hf + 1) * 512], j == 0, j == 7, R=(COT, Wco), W=(PS[b],))
                self.tt(xb[:, hf * 512:(hf + 1) * 512], xb[:, hf * 512:(hf + 1) * 512], pv, ALU.add, R=(PS[b], xb), W=(xb,))
            self.store(xb, self.xrows(t), xb[:, :])
        self.R.barrier()
        self.top = mark

    def phase_f(self, l, last):
        cfg = self.cfg
        ntp, nts = cfg["NTP"], cfg["NTS"]
        mark = self.top
        Wg = self.alloc("Wg", [8, DFF], BF16)
        Wu = self.alloc("Wu", [8, DFF], BF16)
        Wd = self.alloc("Wd", [22, 1024], BF16)
        self.wload(Wg, self.din["w_gate"][l], 8)
        self.wload(Wu, self.din["w_up"][l], 8)
        self.wload(Wd, self.din["w_down"][l], 22)
        AT = self.alloc("AT", [22, 128], BF16)
        SG = [self.alloc("SG%d" % i, [4, 128]) for i in range(2)]
        if last:
            NF = self.alloc("NF", [1024])
            self.load(NF, NF[:, :], self.din["nfin"][:, :])
        PS = self.PS
        XT, hT = self.XT, self.hT
        self.load(XT[0], XT[0][:, :], self.xrows(0))
        for t in range(ntp + nts):
            xb = XT[t % 2]
            if t + 1 < ntp + nts:
                self.load(XT[(t + 1) % 2], XT[(t + 1) % 2][:, :], self.xrows(t + 1))
            self.norm_T(xb[:, :], xb, 4 + l, hT)
            for gi in range(6):
                j0 = gi * 4
                nb = min(4, 22 - j0)
                bg, bu = 1 + 2 * (gi % 2), 2 + 2 * (gi % 2)
                pg, pu = self.psv(bg, [4, 128]), self.psv(bu, [4, 128])
                for jj in range(nb):
                    j = j0 + jj
                    for k in range(8):
                        self.mm(pg[:, jj, :], Wg[:, k, j * 128:(j + 1) * 128], hT[:, k, :], k == 0, k == 7, R=(Wg, hT), W=(PS[bg],))
                    for k in range(8):
                        self.mm(pu[:, jj, :], Wu[:, k, j * 128:(j + 1) * 128], hT[:, k, :], k == 0, k == 7, R=(Wu, hT), W=(PS[bu],))
                sg = SG[gi % 2]
                self.act(sg[:, 0:nb, :], pg[:, 0:nb, :], AF.Silu, R=(PS[bg],), W=(sg,))
                self.tt(AT[:, j0:j0 + nb, :], sg[:, 0:nb, :], pu[:, 0:nb, :], ALU.mult, R=(sg, PS[bu]), W=(AT,))
            for hf in range(2):
                b = 5 + hf
                pv = self.psv(b, [512])
                for j in range(22):
                    self.mm(pv, AT[:, j, :], Wd[:, j, hf * 512:(hf + 1) * 512], j == 0, j == 21, R=(AT, Wd), W=(PS[b],))
                self.tt(xb[:, hf * 512:(hf + 1) * 512], xb[:, hf * 512:(hf + 1) * 512], pv, ALU.add, R=(PS[b], xb), W=(xb,))
            if last:
                XN, SS = self.XN, self.SS
                self.act(XN[:, :], xb[:, :], AF.Square, R=(xb,), W=(XN, SS), accum=SS[:, 0:1])
                self.act(SS[:, 1:2], SS[:, 0:1], AF.Ln, bias=self.epsb[:, 0:1], scale=1.0 / D, R=(SS, self.CF), W=(SS,))
                self.act(SS[:, 2:3], SS[:, 1:2], AF.Exp, scale=-0.5, R=(SS,), W=(SS,))
                self.stt(xb[:, :], xb[:, :], SS[:, 2:3], NF[:, :], ALU.mult, ALU.mult, R=(xb, SS, NF), W=(xb,))
            self.store(xb, self.xrows(t), xb[:, :])
        self.R.barrier()
        self.top = mark

    def declare_io(self):
        cfg = self.cfg
        SEQ, PAST, NSQ = cfg["NTP"] * 128, cfg["PAST"], cfg["NTS"] * 2
        i, o = self.inp, self.outp
        i("xp", [SEQ, D]); i("xs", [NSQ * 64, D]); i("memp", [256, D])
        i("state_a_conv", [2, NSQ, 3, 768]); i("state_a_S", [2, NSQ, 4, 64, 64])
        for n in ("cache_b_k", "cache_b_v", "cache_c_k", "cache_c_v"):
            i(n, [2, NSQ, PAST, 256])
        i("state_d_S", [2, NSQ, 4, 64, 64])
        i("cache_mem_k", [2, NSQ, 256, D]); i("cache_mem_v", [2, NSQ, 256, D])
        i("w_in", [2, D, DIN]); i("w_out", [2, D, D])
        for n in ("w_cq", "w_ck", "w_cv", "w_co"):
            i(n, [2, D, D])
        i("w_gate", [2, D, DFF]); i("w_up", [2, D, DFF]); i("w_down", [2, DFF, D])
        i("nfin", [128, D])
        o("y_p", [SEQ, D]); o("y_s", [NSQ * 64, D])
        o("p_a_conv", [2, 3, 768]); o("p_a_S", [2, 4, 64, 64])
        for n in ("p_b_k", "p_b_v", "p_c_k", "p_c_v"):
            o(n, [2, SEQ, 256])
        o("p_d_S", [2, 4, 64, 64]); o("p_mem_k", [2, 256, D]); o("p_mem_v", [2, 256, D])
        o("s_a_conv", [2, NSQ, 3, 768]); o("s_a_S", [2, NSQ, 4, 64, 64])
        for n in ("s_b_k", "s_b_v", "s_c_k", "s_c_v"):
            o(n, [2, NSQ, 64, 256])
        o("s_d_S", [2, NSQ, 4, 64, 64])

    def build(self):
        import contextlib
        cfg = self.cfg
        with contextlib.ExitStack() as st:
            self.declare_io()
            self.init_mem(st)
            self.load_consts()
            self.XT = [self.alloc("XT%d" % i, [1024]) for i in range(2)]
            self.XN = self.alloc("XN", [1024], BF16)
            self.SS = self.alloc("SS", [4])
            self.hT = self.alloc("hT", [8, 128], BF16)
            phases = cfg.get("phases", "MXF")
            first = True
            for l in range(2):
                if "M" in phases:
                    self.phase_m(l, first)
                    first = False
                if "X" in phases:
                    self.phase_x(l, first)
                    first = False
                if "F" in phases:
                    self.phase_f(l, l == 1)
            self.R.replay()
        return self.nc


FULL_CFG = dict(NTP=32, NTS=2, PAST=4096, phases="MXF")


def make_in_maps(inputs, cfg, ncores=NCORES):
    nsq = cfg["NTS"] * 2
    SEQ = cfg["NTP"] * 128
    cf, cb = K.host_consts()
    constf = np.ascontiguousarray(np.concatenate(list(cf.values()), axis=1).astype(np.float32))
    constb = np.ascontiguousarray(np.concatenate(list(cb.values()), axis=1).astype(np.float32))
    small = K.host_small(inputs)
    nb = inputs["x_prompt"].shape[0]
    maps = []
    f = lambda a: np.ascontiguousarray(np.asarray(a, dtype=np.float32))
    for c in range(ncores):
        b = c % nb
        sl = slice(c * nsq, (c + 1) * nsq)
        m = {"constf": constf, "constb": constb, "smallp": small,
             "nfin": np.ascontiguousarray(np.broadcast_to(np.asarray(inputs["norm_final"], dtype=np.float32)[None, :], (128, D))),
             "xp": f(inputs["x_prompt"][b]), "xs": f(inputs["x_sample"][sl].reshape(nsq * 64, D)),
             "memp": f(inputs["mem_prompt"][b]),
             "state_a_conv": f(inputs["state_a_conv"][:, sl]), "state_a_S": f(inputs["state_a_S"][:, sl]),
             "state_d_S": f(inputs["state_d_S"][:, sl]),
             "cache_mem_k": f(inputs["cache_mem_k"][:, sl].reshape(2, nsq, 256, D)),
             "cache_mem_v": f(inputs["cache_mem_v"][:, sl].reshape(2, nsq, 256, D))}
        for n in ("cache_b_k", "cache_b_v", "cache_c_k", "cache_c_v"):
            m[n] = f(inputs[n][:, sl].reshape(2, nsq, -1, 256))
        for n in ("w_in", "w_out", "w_cq", "w_ck", "w_cv", "w_co", "w_gate", "w_up", "w_down"):
            m[n] = f(inputs[n])
        maps.append(m)
    return maps


def gather(results, cfg, nb, ncores=NCORES):
    nsq = cfg["NTS"] * 2
    SEQ = cfg["NTP"] * 128
    r = results
    P = lambda n: np.stack([r[b][n] for b in range(nb)], axis=0)
    Sx = lambda n: np.concatenate([r[c][n] for c in range(ncores)], axis=1)
    y_p = P("y_p")
    y_s = np.concatenate([r[c]["y_s"].reshape(nsq, 64, D) for c in range(ncores)], axis=0)
    mv = lambda a: np.moveaxis(a, 0, 1)
    outs = [y_p, y_s, mv(P("p_a_conv")), mv(P("p_a_S"))]
    for n in ("p_b_k", "p_b_v", "p_c_k", "p_c_v"):
        outs.append(mv(P(n)).reshape(2, nb, SEQ, 4, 64))
    outs.append(mv(P("p_d_S")))
    for n in ("p_mem_k", "p_mem_v"):
        outs.append(mv(P(n)).reshape(2, nb, 256, 4, 256))
    outs += [Sx("s_a_conv"), Sx("s_a_S")]
    for n in ("s_b_k", "s_b_v", "s_c_k", "s_c_v"):
        outs.append(Sx(n).reshape(2, nsq * ncores, 64, 4, 64))
    outs.append(Sx("s_d_S"))
    return tuple(np.ascontiguousarray(o.astype(np.float32)) for o in outs)


def kernel(**inputs):
    cfg = FULL_CFG
    inputs = {k: np.asarray(v) for k, v in inputs.items()}
    kb = K(cfg)
    nc = kb.build()
    maps = make_in_maps(inputs, cfg)
    res = run_bass_kernel_spmd(nc, maps, core_ids=list(range(NCORES)))
    return gather(res.results, cfg, inputs["x_prompt"].shape[0])
```

```python
import math
import numpy as np
import concourse.bass as bass
import concourse.mybir as mybir
from concourse.bass_utils import run_bass_kernel_spmd

F32 = mybir.dt.float32
BF16 = mybir.dt.bfloat16
AF = mybir.ActivationFunctionType
ALU = mybir.AluOpType
AX = mybir.AxisListType

D = 1024
DFF = 2816
DIN = 3592
NCORES = 8
EPS = 1e-6
SLOPES = [2.0 ** (-8.0 * (h + 1) / 4) for h in range(4)]


class Buf:
    __slots__ = ("ap", "name", "w", "r", "dsem", "dcnt", "lastw", "excl")

    def __init__(self, ap, name, excl=False):
        self.excl = excl
        self.ap = ap
        self.name = name
        self.w = None
        self.r = {}
        self.dsem = None
        self.dcnt = 0

    def __getitem__(self, k):
        return self.ap[k]


class Rec:
    ENG = ("pe", "act", "dve", "pool", "sp")

    def __init__(self, nc):
        self.nc = nc
        self.ops = []
        self.last = {e: None for e in self.ENG}
        self.dbufs = []
        self.pending_barrier = {e: [] for e in self.ENG}

    def _deps(self, reads, writes):
        deps = set()
        for b in reads:
            if b.w is not None:
                deps.add(b.w)
        for b in writes:
            if b.w is not None:
                deps.add(b.w)
            deps.update(b.r.values())
        return deps

    def op(self, eng, fn, reads=(), writes=(), dmabuf=None):
        if any(b.excl for b in reads):
            writes = tuple(writes) + tuple(b for b in reads if b.excl)
            reads = tuple(b for b in reads if not b.excl)
        i = len(self.ops)
        deps = self._deps(reads, writes)
        if self.pending_barrier[eng]:
            deps.update(self.pending_barrier[eng])
            self.pending_barrier[eng] = []
        dval = None
        if dmabuf is not None:
            if dmabuf.dsem is None:
                dmabuf.dsem = True
                self.dbufs.append(dmabuf)
            dmabuf.dcnt += 16
            dval = dmabuf.dcnt
        dl = []
        for j in deps:
            pj = self.ops[j]
            if pj[3] is not None:
                dl.append((j, pj[3], pj[3].dcnt if pj[3] is not dmabuf else pj[3].dcnt - 16))
            else:
                dl.append((j, None, 0))
        self.ops.append((eng, fn, dl, dmabuf, dval))
        for b in reads:
            b.r[eng] = i
        for b in writes:
            b.w = i
            b.r = {}
        self.last[eng] = i
        return i

    def barrier(self):
        lasts = [v for v in self.last.values() if v is not None]
        for e in self.ENG:
            self.pending_barrier[e] = list(lasts) + [b.lastw for b in self.dbufs if getattr(b, "lastw", None) is not None]

    def replay(self):
        nc = self.nc
        ops = self.ops
        n = len(ops)
        need_inc = [False] * n
        for (eng, fn, dl, dmabuf, dval) in ops:
            for (j, db, dv) in dl:
                if db is None and (ops[j][0] != eng or dmabuf is not None or eng != "pe"):
                    need_inc[j] = True
        inc_idx = [0] * n
        cnt = {e: 0 for e in self.ENG}
        for i, o in enumerate(ops):
            if need_inc[i]:
                cnt[o[0]] += 1
                inc_idx[i] = cnt[o[0]]
        import contextlib
        with contextlib.ExitStack() as st:
            esem = {e: st.enter_context(nc.semaphore("sem_" + e)) for e in self.ENG}
            for k, b in enumerate(self.dbufs):
                b.dsem = st.enter_context(nc.semaphore("dsem%d" % k))
            block = st.enter_context(nc.Block())
            per_eng = {e: [i for i, o in enumerate(ops) if o[0] == e] for e in self.ENG}
            dbufs = self.dbufs

            def run(e, ename):
                waited = {x: 0 for x in self.ENG}
                dwaited = {}
                for i in per_eng[ename]:
                    (_, fn, dl, dmabuf, dval) = ops[i]
                    for (j, db, dv) in dl:
                        if db is not None:
                            if dwaited.get(id(db), 0) < dv:
                                e.wait_ge(db.dsem, dv)
                                dwaited[id(db)] = dv
                        else:
                            pe_ = ops[j][0]
                            if (pe_ != ename or dmabuf is not None or ename != "pe") and waited[pe_] < inc_idx[j]:
                                e.wait_ge(esem[pe_], inc_idx[j])
                                waited[pe_] = inc_idx[j]
                    ins = fn(e)
                    if dmabuf is not None:
                        ins.then_inc(dmabuf.dsem, 16)
                    elif need_inc[i]:
                        ins.then_inc(esem[ename], 1)
                if ename == "sp":
                    for b in dbufs:
                        if b.dcnt:
                            e.wait_ge(b.dsem, b.dcnt)

            @block.tensor
            def _(e):
                run(e, "pe")

            @block.scalar
            def _(e):
                run(e, "act")

            @block.vector
            def _(e):
                run(e, "dve")

            @block.gpsimd
            def _(e):
                run(e, "pool")

            @block.sync
            def _(e):
                run(e, "sp")


class K:
    def __init__(self, cfg):
        self.cfg = cfg
        self.nc = bass.Bass("TRN2", target_bir_lowering=False)
        self.R = Rec(self.nc)
        self.din = {}
        self.dout = {}
        self._uid = 0

    def inp(self, name, shape):
        self.din[name] = self.nc.dram_tensor(name, list(shape), F32, kind="ExternalInput").ap()
        return self.din[name]

    def outp(self, name, shape):
        self.dout[name] = self.nc.dram_tensor(name, list(shape), F32, kind="ExternalOutput").ap()
        return self.dout[name]

    def init_mem(self, st):
        nc = self.nc
        self.ARW = 53000
        self.arena = st.enter_context(nc.sbuf_tensor("arena", [128, self.ARW], F32))
        self.top = 0
        self.psb = [st.enter_context(nc.psum_tensor("psb%d" % i, [128, 512], F32)) for i in range(8)]
        self.PS = [Buf(self.psb[i], "ps%d" % i, excl=True) for i in range(8)]

    def alloc(self, name, free_shape, dt=F32, parts=128):
        nel = int(np.prod(free_shape))
        nw = (nel * (4 if dt == F32 else 2) + 3) // 4
        nw = (nw + 7) // 8 * 8
        off = self.top
        self.top += nw
        assert self.top <= self.ARW, ("SBUF overflow", name, self.top)
        ap = self.arena[0:parts, off:off + nw]
        if dt != F32:
            ap = ap.bitcast(dt)
        ap = ap[:, 0:nel]
        if len(free_shape) == 2:
            ap = ap.rearrange("p (a b) -> p a b", b=free_shape[1])
        elif len(free_shape) == 3:
            ap = ap.rearrange("p (a b c) -> p a b c", b=free_shape[1], c=free_shape[2])
        elif len(free_shape) == 4:
            ap = ap.rearrange("p (a b c d) -> p a b c d", b=free_shape[1], c=free_shape[2], d=free_shape[3])
        return Buf(ap, name)

    def psv(self, i, free_shape, dt=F32, parts=128):
        ap = self.psb[i][0:parts, :]
        if dt != F32:
            ap = ap.bitcast(dt)
        nel = int(np.prod(free_shape))
        ap = ap[:, 0:nel]
        if len(free_shape) == 2:
            ap = ap.rearrange("p (a b) -> p a b", b=free_shape[1])
        elif len(free_shape) == 3:
            ap = ap.rearrange("p (a b c) -> p a b c", b=free_shape[1], c=free_shape[2])
        return ap

    def mm(self, out, lhsT, rhs, start=True, stop=True, R=(), W=()):
        self.R.op("pe", lambda e: e.matmul(out, lhsT=lhsT, rhs=rhs, start=start, stop=stop), R, W)

    def tr(self, out, in_, ident, R=(), W=()):
        self.R.op("pe", lambda e: e.transpose(out, in_, ident), R, W)

    def act(self, out, in_, func, bias=0.0, scale=1.0, R=(), W=(), accum=None):
        if accum is None:
            self.R.op("act", lambda e: e.activation(out=out, in_=in_, func=func, bias=bias, scale=scale), R, W)
        else:
            self.R.op("act", lambda e: e.activation(out=out, in_=in_, func=func, bias=bias, scale=scale,
                                                    accum_out=accum), R, W)

    def ts(self, out, in0, s1, s2, op0, op1=None, R=(), W=(), eng="dve"):
        if op1 is None:
            self.R.op(eng, lambda e: e.tensor_scalar(out=out, in0=in0, scalar1=s1, scalar2=None, op0=op0), R, W)
        else:
            self.R.op(eng, lambda e: e.tensor_scalar(out=out, in0=in0, scalar1=s1, scalar2=s2, op0=op0, op1=op1), R, W)

    def tt(self, out, in0, in1, op, R=(), W=(), eng="dve"):
        self.R.op(eng, lambda e: e.tensor_tensor(out=out, in0=in0, in1=in1, op=op), R, W)

    def stt(self, out, in0, scalar, in1, op0, op1, R=(), W=()):
        self.R.op("dve", lambda e: e.scalar_tensor_tensor(out=out, in0=in0, scalar=scalar, in1=in1, op0=op0, op1=op1),
                  R, W)

    def cp(self, out, in_, R=(), W=(), eng="dve"):
        if eng == "act":
            self.R.op("act", lambda e: e.activation(out=out, in_=in_, func=AF.Copy), R, W)
        else:
            self.R.op(eng, lambda e: e.tensor_copy(out=out, in_=in_), R, W)

    def memset(self, ap, val, W=(), eng="pool"):
        self.R.op(eng, lambda e: e.memset(ap, val), (), W)

    def red(self, out, in_, op=ALU.add, R=(), W=()):
        self.R.op("dve", lambda e: e.tensor_reduce(out=out, in_=in_, axis=AX.X, op=op), R, W)

    def dma(self, q, out, in_, buf, R=(), W=(), slow=False):
        if slow:
            i = self.R.op(q, lambda e: e.dma_start(out=out, in_=in_, allow_slow_non_contiguous=True), R, W, dmabuf=buf)
        else:
            i = self.R.op(q, lambda e: e.dma_start(out=out, in_=in_), R, W, dmabuf=buf)
        buf.lastw = i

    def load(self, buf, out, in_, q="sp", slow=False):
        self.dma(q, out, in_, buf, (), (buf,), slow=slow)

    def store(self, buf, out, in_, q="sp"):
        self.dma(q, out, in_, buf, (buf,), ())

    @staticmethod
    def host_consts():
        p = np.arange(128)[:, None].astype(np.float64)
        f = np.arange(128)[None, :].astype(np.float64)
        cf = {}
        cf["ident"] = (p == f).astype(np.float32)
        cf["cmask"] = (p < f).astype(np.float32)
        dg = np.zeros((128, 4, 128), np.float32)
        for h in range(4):
            b = -SLOPES[h] * np.abs(f - p) + SLOPES[h] * f
            b = np.where((p >= 64) & (f < 64), -30000.0, b)
            dg[:, h, :] = b
        cf["diagb"] = dg.reshape(128, 512)
        ds_ = np.zeros((128, 4, 64), np.float32)
        pl = (np.arange(128) % 64)[:, None].astype(np.float64)
        f64 = np.arange(64)[None, :].astype(np.float64)
        for h in range(4):
            ds_[:, h, :] = -SLOPES[h] * np.abs(f64 - pl) + SLOPES[h] * f64
        cf["diags"] = ds_.reshape(128, 256)
        bs = np.zeros((128, 4, 33), np.float32)
        for h in range(4):
            for n in range(33):
                bs[:, h, n] = SLOPES[h] * (np.arange(128) - 128.0 * n)
        cf["bias"] = bs.reshape(128, 132)
        j = (np.arange(128) % 64)[:, None]
        i = np.arange(64)[None, :]
        cf["tri64"] = (j <= i).astype(np.float32)
        cf["mneg_ui"] = np.tile(np.where(j <= i, 0.0, -30000.0).astype(np.float32)[:, None, :], (1, 4, 1)).reshape(128, 256)
        cf["mpos_sl"] = np.tile(np.where(i < j, 0.0, 30000.0).astype(np.float32)[:, None, :], (1, 4, 1)).reshape(128, 256)
        cf["mask_ui"] = np.tile((j <= i).astype(np.float32)[:, None, :], (1, 4, 1)).reshape(128, 256)
        cf["blkones"] = ((np.arange(128)[:, None] // 64) == (np.arange(128)[None, :] // 64)).astype(np.float32)
        cf["ones"] = np.ones((128, 128), np.float32)
        cf["ident64"] = np.tile(np.eye(64, dtype=np.float32), (2, 1))
        cf["epsc"] = np.full((128, 2), EPS, np.float32)
        cb = {}
        cb["identb"] = cf["ident"]
        cb["utri"] = (p >= f).astype(np.float32)
        cb["onesb"] = np.ones((128, 128), np.float32)
        cb["zerob"] = np.zeros((128, 512), np.float32)
        return cf, cb

    def load_consts(self):
        cf, cb = self.host_consts()
        self.cf_off, o = {}, 0
        for k, v in cf.items():
            self.cf_off[k] = (o, v.shape[1])
            o += v.shape[1]
        self.ncf = o
        self.cb_off, o = {}, 0
        for k, v in cb.items():
            self.cb_off[k] = (o, v.shape[1])
            o += v.shape[1]
        self.ncb = o
        dcf = self.inp("constf", [128, self.ncf])
        dcb = self.inp("constb", [128, self.ncb])
        self.CF = self.alloc("CF", [self.ncf])
        self.CB = self.alloc("CB", [self.ncb], BF16)
        self.load(self.CF, self.CF[:, :], dcf[:, :])
        self.load(self.CB, self.CB[:, :], dcb[:, :], q="pool")
        self.epsb = self.c("epsc")
        ns = self.small_layout()
        dsm = self.inp("smallp", [128, ns])
        self.SM = self.alloc("SM", [ns])
        self.load(self.SM, self.SM[:, :], dsm[:, :])

    def c(self, name, parts=128, p0=0):
        o, n = self.cf_off[name]
        return self.CF[p0:p0 + parts, o:o + n]

    def cbf(self, name, parts=128, p0=0):
        o, n = self.cb_off[name]
        return self.CB[p0:p0 + parts, o:o + n]

    SMALL = [("nw", 8 * 8), ("convw", 2 * 6 * 4), ("dlb", 2 * 2), ("anorm", 2 * 256), ("bnorm", 2 * 256),
             ("dnorm", 2 * 256), ("alog", 2 * 4), ("dtb", 2 * 4), ("lamv", 2 * 4 * 32)]

    def small_layout(self):
        self.sm_off, o = {}, 0
        for k, n in self.SMALL:
            self.sm_off[k] = (o, n)
            o += n
        return o

    @classmethod
    def host_small(cls, inp):
        out = []
        norms = [inp["norm_mix"][0], inp["norm_mix"][1], inp["norm_cross"][0], inp["norm_cross"][1],
                 inp["norm_ffn"][0], inp["norm_ffn"][1], inp["norm_memtok"][0], inp["norm_memtok"][1]]
        nw = np.stack([n.reshape(8, 128).T for n in norms], axis=1)
        out.append(nw.reshape(128, 64))
        cw = inp["a_conv_w"].reshape(2, 4, 6, 128).transpose(3, 0, 2, 1)
        out.append(cw.reshape(128, 48))
        dl = inp["d_lb"].reshape(2, 2, 128).transpose(2, 0, 1)
        out.append(dl.reshape(128, 4))
        for nm in ("a_norm", "b_norm", "d_norm"):
            v = np.tile(inp[nm][:, None, :], (1, 4, 1)).reshape(1, 512)
            out.append(np.broadcast_to(v, (128, 512)))
        out.append(np.broadcast_to(inp["a_A_log"].reshape(1, 8), (128, 8)))
        out.append(np.broadcast_to(inp["a_dt_bias"].reshape(1, 8), (128, 8)))
        lv = np.stack([inp["b_lam_q1"], inp["b_lam_k1"], inp["b_lam_q2"], inp["b_lam_k2"]], axis=1)
        out.append(np.broadcast_to(lv.reshape(1, 256), (128, 256)))
        return np.ascontiguousarray(np.concatenate(out, axis=1).astype(np.float32))

    def sm(self, name, parts=128, p0=0):
        o, n = self.sm_off[name]
        return self.SM[p0:p0 + parts, o:o + n]

    def wload(self, W, dram2d, nk, q="pool"):
        for k in range(nk):
            self.load(W, W[:, k, :], dram2d[k * 128:(k + 1) * 128, :], q=q)

    def norm_T(self, x_ap, xbuf, nw_idx, hT, tcols=slice(0, 128), psb=0):
        XN, SS = self.XN, self.SS
        PSb = self.PS[psb]
        self.act(XN[:, :], x_ap, AF.Square, R=(xbuf,), W=(XN, SS), accum=SS[:, 0:1])
        self.act(SS[:, 1:2], SS[:, 0:1], AF.Ln, bias=self.epsb[:, 0:1], scale=1.0 / D, R=(SS, self.CF), W=(SS,))
        self.act(SS[:, 2:3], SS[:, 1:2], AF.Exp, scale=-0.5, R=(SS,), W=(SS,))
        self.ts(XN[:, :], x_ap, SS[:, 2:3], None, ALU.mult, R=(xbuf, SS), W=(XN,))
        pv = self.psv(psb, [8, 128], BF16)
        for k in range(8):
            self.tr(pv[:, k, :], XN[:, k * 128:(k + 1) * 128], self.cbf("identb"), R=(XN, self.CB), W=(PSb,))
        nw = self.sm("nw")
        for k in range(8):
            o = nw_idx * 8 + k
            self.act(hT[:, k, tcols], pv[:, k, :], AF.Copy, scale=nw[:, o:o + 1], R=(PSb, self.SM), W=(hT,))

    def xrows(self, t):
        ntp = self.cfg["NTP"]
        if t < ntp:
            return self.dout["y_p"][t * 128:(t + 1) * 128, :]
        return self.dout["y_s"][(t - ntp) * 128:(t - ntp + 1) * 128, :]

    def xin_rows(self, t):
        ntp = self.cfg["NTP"]
        if t < ntp:
            return self.din["xp"][t * 128:(t + 1) * 128, :]
        return self.din["xs"][(t - ntp) * 128:(t - ntp + 1) * 128, :]

    def bmul(self, out, in0, sc, nh, R=(), W=()):
        for h in range(nh):
            self.ts(out[:, h, :], in0[:, h, :], sc[:, h:h + 1], None, ALU.mult, R=R, W=W)

    def rsqrt_act(self, out, in_, scale, R=(), W=()):
        P = out.shape[0]
        self.act(out, in_, AF.Ln, bias=self.epsb[0:P, 0:1], scale=scale, R=tuple(R) + (self.CF,), W=W)
        self.act(out, out, AF.Exp, scale=-0.5, R=W, W=W)

    def gated_norm_T(self, O, nq, normw, gate_ap, gate_bufs, OT, blk0, cols, post_scale=None):
        G = self.G
        SQ, YB, ST = G[10], G[11], self.SSM
        o = O[0:nq, 0:4, :]
        sq = SQ[0:nq, 0:256].rearrange("p (h e) -> p h e", e=64)
        self.tt(sq, o, o, ALU.mult, R=(O,), W=(SQ,))
        self.red(ST[0:nq, 0:4], sq, R=(SQ,), W=(ST,))
        self.rsqrt_act(ST[0:nq, 0:4], ST[0:nq, 0:4], 1.0 / 64.0, R=(ST,), W=(ST,))
        self.bmul(sq, o, ST[0:nq, 0:4], 4, R=(O, ST), W=(SQ,))
        nw = normw.rearrange("p (h e) -> p h e", e=64)
        if post_scale is None:
            self.tt(sq, sq, nw, ALU.mult, R=(SQ, self.SM), W=(SQ,))
        else:
            self.stt(sq, sq, post_scale, nw, ALU.mult, ALU.mult, R=(SQ, self.SM), W=(SQ,))
        yb = YB[0:nq, 256:384].bitcast(BF16)
        if gate_ap is not None:
            GT = G[12]
            gt = GT[0:nq, 0:256]
            self.act(gt, gate_ap, AF.Silu, R=gate_bufs, W=(GT,))
            self.tt(yb, SQ[0:nq, 0:256], gt, ALU.mult, R=(SQ, GT), W=(YB,))
        else:
            self.cp(yb, SQ[0:nq, 0:256], R=(SQ,), W=(YB,))
        pv = self.psv(0, [2, 128], BF16)
        for b in range(2):
            self.tr(pv[:, b, 0:nq], yb[:, b * 128:(b + 1) * 128], self.cbf("identb", parts=nq)[:, 0:nq], R=(YB, self.CB), W=(self.PS[0],))
        self.cp(OT[:, blk0:blk0 + 2, cols], pv[:, :, 0:nq], R=(self.PS[0],), W=(OT,), eng="act")


    def phase_m(self, l, first_phase):
        cfg = self.cfg
        ntp, nts, PAST = cfg["NTP"], cfg["NTS"], cfg["PAST"]
        NKP = PAST // 128
        mix = cfg.get("mix", "ABCD")
        mark = self.top
        PS = self.PS
        XT, hT = self.XT, self.hT
        Win = self.alloc("Win", [8, DIN], BF16)
        self.wload(Win, self.din["w_in"][l], 8)
        WO = [self.alloc("WO%d" % i, [1024], BF16) for i in range(2)]
        KTB = self.alloc("KTB", [2, ntp * 128], BF16)
        KTC = self.alloc("KTC", [2, ntp * 128], BF16)
        VB = self.alloc("VB", [ntp, 4, 65], BF16)
        VC = self.alloc("VC", [ntp, 256], BF16)
        CV = self.alloc("CV", [6, 134])
        U = self.alloc("U", [6, 128])
        QZ = self.alloc("QZ", [2, 4, 128], BF16)
        CQ = self.alloc("CQ", [2, 2, 128], BF16)
        DQ = self.alloc("DQ", [2, 128])
        DF = self.alloc("DF", [2, 128])
        STB = self.alloc("STB", [512])
        STC = STB
        DIG = self.alloc("DIG", [1, 512])
        AZ = self.alloc("AZ", [1, 264])
        OT = self.alloc("OT", [8, 128], BF16)
        SA = self.alloc("SA", [4, 64])
        SD = self.alloc("SD", [4, 64])
        self.SSM = self.alloc("SSM", [16])
        SM1 = self.alloc("SM1", [64])
        self.SM1b = SM1
        KTBs = self.alloc("KTBs", [2, 128], BF16)
        KTCs = self.alloc("KTCs", [2, 128], BF16)
        VBs = self.alloc("VBs", [2, 4, 65], BF16)
        VCs = self.alloc("VCs", [2, 256], BF16)
        stg = []
        for i in range(2):
            kb_ = self.alloc("skb%d" % i, [256], BF16)
            stg.append(dict(kb=kb_, vb=self.alloc("svb%d" % i, [4, 65], BF16),
                            kc=kb_, vc=self.alloc("svc%d" % i, [256], BF16),
                            kt=self.alloc("skt%d" % i, [2, 128], BF16)))
        self.G = [self.alloc("G%d" % i, [512]) for i in range(9)]
        self.G.append(self.alloc("G9", [16]))
        self.G.append(self.alloc("G10", [512]))
        g12 = self.alloc("G12", [512])
        self.G += [g12, g12, self.alloc("G13", [512])]
        G = self.G
        cf, sm = self.c, self.sm
        self.memset(QZ[:, :, :, :], 0.0, W=(QZ,))
        self.memset(CQ[:, :, :, :], 0.0, W=(CQ,))
        self.memset(VB[:, :, :, 64:65], 1.0, W=(VB,))
        self.memset(VBs[:, :, :, 64:65], 1.0, W=(VBs,))
        for i in range(2):
            self.memset(stg[i]["vb"][:, :, 64:65], 1.0, W=(stg[i]["vb"],))
        self.memset(SA[:, :, :], 0.0, W=(SA,))
        self.memset(SD[:, :, :], 0.0, W=(SD,))
        self.memset(CV[:, :, 0:3], 0.0, W=(CV,))
        self.act(SM1[:, 0:4], sm("alog")[:, l * 4:(l + 1) * 4], AF.Exp, R=(self.SM,), W=(SM1,))
        self.ts(SM1[:, 0:4], SM1[:, 0:4], -1.0, None, ALU.mult, R=(SM1,), W=(SM1,))
        if l == 0:
            self.memset(SM1[:, 4:6], 0.0, W=(SM1,), eng="dve")
        else:
            dl = sm("dlb")
            self.tt(SM1[:, 4:6], dl[:, 0:2], dl[:, 2:4], ALU.subtract, R=(self.SM,), W=(SM1,))
            self.act(SM1[:, 4:6], SM1[:, 4:6], AF.Exp, R=(SM1,), W=(SM1,))
            self.ts(SM1[:, 4:6], SM1[:, 4:6], 1.0, None, ALU.add, R=(SM1,), W=(SM1,))
            self.R.op("dve", lambda e: e.reciprocal(out=SM1[:, 4:6], in_=SM1[:, 4:6]), (SM1,), (SM1,))
        lv = sm("lamv")[:, l * 128:(l + 1) * 128].rearrange("p (a b) -> p a b", b=32)
        lt = G[0][:, 0:64].rearrange("p (a b) -> p a b", b=32)
        self.tt(lt[:, 0, :], lv[:, 0, :], lv[:, 1, :], ALU.mult, R=(self.SM,), W=(G[0],))
        self.tt(lt[:, 1, :], lv[:, 2, :], lv[:, 3, :], ALU.mult, R=(self.SM,), W=(G[0],))
        self.red(SM1[:, 10:12], lt, R=(G[0],), W=(SM1,))
        self.act(SM1[:, 10:12], SM1[:, 10:12], AF.Exp, R=(SM1,), W=(SM1,))
        lam_init = 0.8 - 0.6 * math.exp(-0.3 * l)
        self.tt(SM1[:, 8:9], SM1[:, 10:11], SM1[:, 11:12], ALU.subtract, R=(SM1,), W=(SM1,))
        self.ts(SM1[:, 9:10], SM1[:, 8:9], lam_init, -1.0, ALU.add, ALU.mult, R=(SM1,), W=(SM1,))
        NEGA, LB, NLAM = SM1[:, 0:4], SM1[:, 4:6], SM1[:, 9:10]
        identf, identb = cf("ident"), self.cbf("identb")

        def ldx(t):
            src = self.xin_rows(t) if first_phase else self.xrows(t)
            self.load(XT[t % 2], XT[t % 2][:, :], src)

        def wo_load(slot, k):
            self.load(WO[slot], WO[slot][:, :], self.din["w_out"][l, k * 128:(k + 1) * 128, :], q="pool")

        ev = [0]

        def evac(out, in_, R, W, scale=None):
            ev[0] += 1
            if ev[0] % 2:
                self.act(out, in_, AF.Copy, scale=(1.0 if scale is None else scale), R=R, W=W)
            elif scale is None:
                self.cp(out, in_, R=R, W=W)
            else:
                self.ts(out, in_, scale, None, ALU.mult, R=R, W=W)

        pb = [0]

        def fm_block(c0, rhs_cols=slice(0, 128)):
            pb[0] += 1
            b = 1 + pb[0] % 2
            n = rhs_cols.stop - rhs_cols.start
            pv = self.psv(b, [n])
            for k in range(8):
                self.mm(pv, Win[:, k, c0:c0 + 128], hT[:, k, rhs_cols], k == 0, k == 7, R=(Win, hT), W=(PS[b],))
            return pv, PS[b]

        def tm_group(c0, n, tok=slice(0, 128)):
            pb[0] += 1
            b = 1 + pb[0] % 2
            m = tok.stop - tok.start
            pv = self.psv(b, [n], parts=m)
            for k in range(8):
                self.mm(pv, hT[:, k, tok], Win[:, k, c0:c0 + n], k == 0, k == 7, R=(Win, hT), W=(PS[b],))
            return pv, PS[b]

        ldx(0)
        for t in range(ntp + nts):
            is_s = t >= ntp
            xb = XT[t % 2]
            if t + 1 < ntp + nts:
                ldx(t + 1)
            for k in range(2):
                wo_load(k, k)
            self.norm_T(xb[:, :], xb, l, hT)
            tcols = slice(t * 128, (t + 1) * 128)
            if is_s:
                for s in range(2):
                    sq = (t - ntp) * 2 + s
                    for bq in range(6):
                        self.load(CV, CV[:, bq, s * 67:s * 67 + 3],
                                  self.din["state_a_conv"][l, sq][:, bq * 128:(bq + 1) * 128].rearrange("t p -> p t"), q="sp", slow=True)
            for i in range(6):
                pv, pbuf = fm_block(i * 128)
                if is_s:
                    evac(CV[:, i, 3:67], pv[:, 0:64], (pbuf,), (CV,))
                    evac(CV[:, i, 70:134], pv[:, 64:128], (pbuf,), (CV,))
                else:
                    evac(CV[:, i, 3:131], pv, (pbuf,), (CV,))
            for p in range(2):
                pv, pbuf = fm_block(1032 + p * 128)
                for (slot, r0) in ((0, 0), (1, 32), (2, 64), (3, 96)):
                    evac(QZ[r0:r0 + 32, p, slot, :], pv[r0:r0 + 32, :], (pbuf,), (QZ,), scale=32 ** -0.5)
                pv, pbuf = fm_block(1288 + p * 128)
                evac(KTBs[:, p, :] if is_s else KTB[:, p, tcols], pv, (pbuf,), (KTBs if is_s else KTB,))
                pv, pbuf = fm_block(1800 + p * 128)
                for hl in range(2):
                    evac(CQ[hl * 64:(hl + 1) * 64, p, hl, :], pv[hl * 64:(hl + 1) * 64, :], (pbuf,), (CQ,), scale=0.125)
                pv, pbuf = fm_block(2056 + p * 128)
                evac(KTCs[:, p, :] if is_s else KTC[:, p, tcols], pv, (pbuf,), (KTCs if is_s else KTC,))
                pv, pbuf = fm_block(2568 + p * 128)
                evac(DQ[:, p, :], pv, (pbuf,), (DQ,), scale=0.125)
                pv, pbuf = fm_block(2824 + p * 128)
                evac(DF[:, p, :], pv, (pbuf,), (DF,))
            for (c0, ST, Vres, Vsm, okn, ovn, is_b) in ((1288, STB, VB, VBs, "b_k", "b_v", True), (2056, STC, VC, VCs, "c_k", "c_v", False)):
                if is_s:
                    for sg in range(2):
                        sq = (t - ntp) * 2 + sg
                        pv, pbuf = tm_group(c0, 512, slice(sg * 64, (sg + 1) * 64))
                        self.act(ST[0:64, :], pv, AF.Copy, R=(pbuf,), W=(ST,))
                        if is_b:
                            self.cp(Vsm[0:64, sg, :, 0:64], ST[0:64, 256:512].rearrange("p (h e) -> p h e", e=64), R=(ST,), W=(Vsm,))
                        else:
                            self.cp(Vsm[0:64, sg, :], ST[0:64, 256:512], R=(ST,), W=(Vsm,))
                        self.store(ST, self.dout["s_" + okn][l, sq, :, :], ST[0:64, 0:256])
                        self.store(ST, self.dout["s_" + ovn][l, sq, :, :], ST[0:64, 256:512])
                else:
                    pv, pbuf = tm_group(c0, 512)
                    self.act(ST[:, :], pv, AF.Copy, R=(pbuf,), W=(ST,))
                    if is_b:
                        self.cp(Vres[:, t, :, 0:64], ST[:, 256:512].rearrange("p (h e) -> p h e", e=64), R=(ST,), W=(Vres,))
                    else:
                        self.cp(Vres[:, t, :], ST[:, 256:512], R=(ST,), W=(Vres,))
                    self.store(ST, self.dout["p_" + okn][l, tcols, :], ST[:, 0:256])
                    self.store(ST, self.dout["p_" + ovn][l, tcols, :], ST[:, 256:512])
            if is_s or t == ntp - 1:
                for (c0, n) in ((0, 512), (512, 256)):
                    pv, pbuf = tm_group(c0, n)
                    self.act(G[0][:, 0:n], pv, AF.Copy, R=(pbuf,), W=(G[0],))
                    if is_s:
                        for s in range(2):
                            sq = (t - ntp) * 2 + s
                            self.store(G[0], self.dout["s_a_conv"][l, sq, :, c0:c0 + n], G[0][s * 64 + 61:s * 64 + 64, 0:n])
                    else:
                        self.store(G[0], self.dout["p_a_conv"][l, :, c0:c0 + n], G[0][125:128, 0:n])
            segs = [(0, 64, 0), (67, 64, 64)] if is_s else [(0, 128, 0)]
            cw = sm("convw")[:, l * 24:(l + 1) * 24].rearrange("p (b t) -> p b t", t=4)
            for i in range(6):
                for (o, n, oo) in segs:
                    dst = U[:, i, oo:oo + n]
                    self.ts(dst, CV[:, i, o + 3:o + 3 + n], cw[:, i, 3:4], None, ALU.mult, R=(CV, self.SM), W=(U,))
                    for tap in (2, 1, 0):
                        self.stt(dst, CV[:, i, o + tap:o + tap + n], cw[:, i, tap:tap + 1], dst, ALU.mult, ALU.add,
                                 R=(CV, self.SM, U), W=(U,))
                if not is_s:
                    self.cp(CV[:, i, 0:3], CV[:, i, 128:131], R=(CV,), W=(CV,), eng="pool")
            self.act(U[:, :, :], U[:, :, :], AF.Silu, R=(U,), W=(U,))
            for i in range(4):
                sqv = G[0][:, 0:128]
                self.tt(sqv, U[:, i, :], U[:, i, :], ALU.mult, R=(U,), W=(G[0],))
                pv = self.psv(7, [128])
                self.mm(pv, cf("blkones"), sqv, R=(self.CF, G[0]), W=(PS[7],))
                rv = G[1][:, 0:128]
                self.rsqrt_act(rv, pv, 1.0, R=(PS[7],), W=(G[1],))
                if i < 2:
                    self.stt(U[:, i, :], U[:, i, :], 0.125, rv, ALU.mult, ALU.mult, R=(U, G[1]), W=(U,))
                else:
                    self.tt(U[:, i, :], U[:, i, :], rv, ALU.mult, R=(U, G[1]), W=(U,))
            for c in range(2):
                tok = slice(c * 64, (c + 1) * 64)
                pv, pbuf = tm_group(3080, 512, tok)
                evac(DIG[0:64, 0, :], pv, (pbuf,), (DIG,))
                pv, pbuf = tm_group(768, 264, tok)
                evac(AZ[0:64, 0, :], pv, (pbuf,), (AZ,))
                if "A" in mix:
                    self.mix_A(l, t, c, U, AZ, SA, OT, NEGA)
                else:
                    self.memset(OT[:, 0:2, c * 64:(c + 1) * 64], 0.0, W=(OT,))
                if "D" in mix:
                    self.mix_D(l, t, c, DQ, DF, DIG, SD, OT, LB)
                else:
                    self.memset(OT[:, 6:8, c * 64:(c + 1) * 64], 0.0, W=(OT,))
            if "B" in mix:
                self.mix_B(l, t, QZ, KTB, VB, KTBs, VBs, stg, OT, NLAM, lam_init)
            else:
                self.memset(OT[:, 2:4, :], 0.0, W=(OT,))
            if "C" in mix:
                self.mix_C(l, t, CQ, KTC, VC, KTCs, VCs, stg, OT)
            else:
                self.memset(OT[:, 4:6, :], 0.0, W=(OT,))
            if "T" in mix:
                YBt = G[11]
                ybt = YBt[0:128, 256:384].bitcast(BF16)
                self.memset(ybt, 0.5, W=(YBt,))
                self.to_OT(YBt, ybt, 128, OT, 4, slice(0, 128))
            if "U" in mix:
                YBt = G[11]
                ybt = YBt[0:64, 256:384].bitcast(BF16)
                self.memset(ybt, 0.5, W=(YBt,))
                self.to_OT(YBt, ybt, 64, OT, 4, slice(0, 64))
            pv0, pv1 = self.psv(1, [512]), self.psv(2, [512])
            for j in range(8):
                w = WO[j % 2]
                self.mm(pv0, OT[:, j, :], w[:, 0:512], j == 0, j == 7, R=(OT, w), W=(PS[1],))
                self.mm(pv1, OT[:, j, :], w[:, 512:1024], j == 0, j == 7, R=(OT, w), W=(PS[2],))
                if j < 6:
                    wo_load(j % 2, j + 2)
            self.tt(xb[:, 0:512], xb[:, 0:512], pv0, ALU.add, R=(PS[1], xb), W=(xb,))
            self.tt(xb[:, 512:1024], xb[:, 512:1024], pv1, ALU.add, R=(PS[2], xb), W=(xb,))
            self.store(xb, self.xrows(t), xb[:, :])
        self.R.barrier()
        self.top = mark


    def state_io(self, l, t, c, S, in_name, out_p, out_s, load):
        ntp = self.cfg["NTP"]
        if t >= ntp:
            sq = (t - ntp) * 2 + c
            for h in range(4):
                hb = (h % 2) * 64
                if load:
                    self.load(S, S[hb:hb + 64, h, :], self.din[in_name][l, sq, h])
                else:
                    self.store(S, self.dout[out_s][l, sq, h], S[hb:hb + 64, h, :])
        elif (not load) and t == ntp - 1 and c == 1:
            for h in range(4):
                hb = (h % 2) * 64
                self.store(S, self.dout[out_p][l, h], S[hb:hb + 64, h, :])

    def mix_D(self, l, t, c, DQ, DF, DIG, SD, OT, LB):
        G, PS, cf = self.G, self.PS, self.c
        cols = slice(c * 64, (c + 1) * 64)
        self.state_io(l, t, c, SD, "state_d_S", "p_d_S", "s_d_S", True)
        if "d0" in self.cfg.get("dbg", ""):
            self.memset(OT[:, 6:8, cols], 0.0, W=(OT,))
            return
        KD, QB, KDEC = (G[i][:, 0:128].rearrange("p (a b) -> p a b", b=64) for i in (1, 2, 3))
        QD = G[0][:, 0:256].rearrange("p (a b) -> p a b", b=64)
        self.memset(QD, 0.0, W=(G[0],))
        EBL = G[5]
        ones = cf("ones")[:, 0:64]

        def chain(p, TB):
            T = [TB[:, i * 64:(i + 1) * 64] for i in range(8)]
            fl, q = DF[:, p, cols], DQ[:, p, cols]
            e1, num, den, lf, bc, ta, tb = T[0], T[1], T[2], T[3], T[4], T[5], T[6]
            self.act(e1, fl, AF.Exp, scale=-1.0, R=(DF,), W=(TB,)); yield
            self.ts(num, e1, LB[:, p:p + 1], 1.0, ALU.mult, ALU.add, R=(TB, self.SM1b), W=(TB,)); yield
            self.ts(den, e1, 1.0, None, ALU.add, R=(TB,), W=(TB,)); yield
            self.R.op("dve", lambda e, den=den: e.reciprocal(out=den, in_=den), (TB,), (TB,)); yield
            self.tt(num, num, den, ALU.mult, R=(TB,), W=(TB,)); yield
            self.act(lf, num, AF.Ln, R=(TB,), W=(TB,)); yield
            self.ts(num, num, -1.0, 1.0, ALU.mult, ALU.add, R=(TB,), W=(TB,)); yield
            self.R.op("dve", lambda e, bc=bc, lf=lf: e.tensor_tensor_scan(out=bc, data0=ones, data1=lf, initial=0.0,
                                                                          op0=ALU.mult, op1=ALU.add),
                      (TB, self.CF), (TB,)); yield
            mid, bl = bc[:, 31:32], bc[:, 63:64]
            self.ts(ta, bc, mid, 80.0, ALU.subtract, ALU.min, R=(TB,), W=(TB,)); yield
            self.act(ta, ta, AF.Exp, R=(TB,), W=(TB,)); yield
            for hl in range(2):
                hs = slice(hl * 64, (hl + 1) * 64)
                self.tt(QD[hs, 2 * p + hl, :], q[hs, :], ta[hs, :], ALU.mult, R=(DQ, TB), W=(G[0],))
            yield
            self.ts(tb, bc, -1.0, mid, ALU.mult, ALU.add, R=(TB,), W=(TB,)); yield
            self.ts(tb, tb, 80.0, None, ALU.min, R=(TB,), W=(TB,)); yield
            self.act(tb, tb, AF.Exp, R=(TB,), W=(TB,)); yield
            self.tt(KD[:, p, :], num, tb, ALU.mult, R=(TB,), W=(G[1],)); yield
            self.act(ta, bc, AF.Exp, R=(TB,), W=(TB,)); yield
            self.tt(QB[:, p, :], q, ta, ALU.mult, R=(DQ, TB), W=(G[2],)); yield
            self.act(tb, bc, AF.Exp, scale=-1.0, bias=bl, R=(TB,), W=(TB,)); yield
            self.tt(KDEC[:, p, :], num, tb, ALU.mult, R=(TB,), W=(G[3],)); yield
            self.act(EBL[:, p:p + 1], bl, AF.Exp, R=(TB,), W=(G[5],)); yield

        gens = [chain(0, G[4]), chain(1, G[13])]
        while gens:
            for g in list(gens):
                try:
                    next(g)
                except StopIteration:
                    gens.remove(g)
        if "d1" in self.cfg.get("dbg", ""):
            self.memset(OT[:, 6:8, cols], 0.0, W=(OT,))
            return
        pa = self.psv(7, [4, 64], parts=64)
        for h in range(4):
            p, hb = h // 2, (h % 2) * 64
            self.mm(pa[:, h, :], KD[:, p, :], QD[:, h, :], R=(G[0], G[1]), W=(PS[7],))
        ATD = G[6][0:64, 0:256].rearrange("p (h e) -> p h e", e=64)
        self.tt(ATD, pa, cf("mask_ui", 64).rearrange("p (h e) -> p h e", e=64), ALU.mult, R=(PS[7], self.CF), W=(G[6],))
        if "d2" in self.cfg.get("dbg", ""):
            self.memset(OT[:, 6:8, cols], 0.0, W=(OT,))
            return
        pk = self.psv(3, [2, 128], parts=64)
        for p in range(2):
            self.tr(pk[:, p, :], KDEC[:, p, :], cf("ident"), R=(G[3], self.CF), W=(PS[3],))
        KDT = G[7][0:64, 0:256].rearrange("p (h e) -> p h e", e=64)
        self.act(KDT, pk.rearrange("p a (b e) -> p (a b) e", e=64), AF.Copy, R=(PS[3],), W=(G[7],))
        if "d3" in self.cfg.get("dbg", ""):
            self.memset(OT[:, 6:8, cols], 0.0, W=(OT,))
            return
        po = self.psv(4, [4, 64], parts=64)
        for h in range(4):
            p, hb = h // 2, (h % 2) * 64
            self.mm(po[:, h, :], ATD[:, h, :], DIG[0:64, 0, h * 64:(h + 1) * 64], True, False, R=(G[6], DIG), W=(PS[4],))
            self.mm(po[:, h, :], QB[:, p, :], SD[:, h, :], False, True, R=(G[2], SD), W=(PS[4],))
        pn = self.psv(5, [2, 64])
        for h in range(4):
            p, hb = h // 2, (h % 2) * 64
            self.mm(pn[hb:hb + 64, p, :], KDT[:, h, :], DIG[0:64, 0, h * 64:(h + 1) * 64], R=(G[7], DIG), W=(PS[5],))
        if "d4" in self.cfg.get("dbg", ""):
            self.memset(OT[:, 6:8, cols], 0.0, W=(OT,))
            return
        for h in range(4):
            p, hs = h // 2, slice((h % 2) * 64, (h % 2) * 64 + 64)
            self.stt(SD[hs, h, :], SD[hs, h, :], EBL[hs, p:p + 1], pn[hs, p, :], ALU.mult, ALU.add, R=(SD, G[5], PS[5]), W=(SD,))
        O = G[8]
        self.act(O[0:64, 0:256].rearrange("p (h e) -> p h e", e=64), po, AF.Copy, R=(PS[4],), W=(G[8],))
        if "d5" in self.cfg.get("dbg", ""):
            self.memset(OT[:, 6:8, cols], 0.0, W=(OT,))
            return
        Ov = Buf(O[0:64, 0:256].rearrange("p (h e) -> p h e", e=64), "Ov")
        self.gated_norm_T2(G[8], 64, self.sm("dnorm")[0:64, l * 256:(l + 1) * 256], DIG[0:64, 0, 256:512], (DIG,), OT, 6, cols)
        self.state_io(l, t, c, SD, "state_d_S", "p_d_S", "s_d_S", False)

    def gated_norm_T2(self, Obuf, nq, normw, gate_ap, gate_bufs, OT, blk0, cols, post_scale=None):
        G = self.G
        SQ, YB, ST = G[10], G[11], self.SSM
        o = Obuf[0:nq, 0:256].rearrange("p (h e) -> p h e", e=64)
        sq = SQ[0:nq, 0:256].rearrange("p (h e) -> p h e", e=64)
        self.tt(sq, o, o, ALU.mult, R=(Obuf,), W=(SQ,))
        self.red(ST[0:nq, 0:4], sq, R=(SQ,), W=(ST,))
        self.rsqrt_act(ST[0:nq, 0:4], ST[0:nq, 0:4], 1.0 / 64.0, R=(ST,), W=(ST,))
        self.bmul(sq, o, ST[0:nq, 0:4], 4, R=(Obuf, ST), W=(SQ,))
        nw = normw.rearrange("p (h e) -> p h e", e=64)
        if post_scale is None:
            self.tt(sq, sq, nw, ALU.mult, R=(SQ, self.SM), W=(SQ,))
        else:
            self.stt(sq, sq, post_scale, nw, ALU.mult, ALU.mult, R=(SQ, self.SM), W=(SQ,))
        yb = YB[0:nq, 256:384].bitcast(BF16)
        if gate_ap is not None:
            GT = G[12]
            gt = GT[0:nq, 0:256]
            self.act(gt, gate_ap, AF.Silu, R=gate_bufs, W=(GT,))
            self.tt(yb, SQ[0:nq, 0:256], gt, ALU.mult, R=(SQ, GT), W=(YB,))
        else:
            self.cp(yb, SQ[0:nq, 0:256], R=(SQ,), W=(YB,))
        self.to_OT(YB, yb, nq, OT, blk0, cols)

    def to_OT(self, YB, yb, nq, OT, blk0, cols):
        pv = self.psv(0, [2, 128], BF16)
        for b in range(2):
            self.tr(pv[:, b, 0:nq], yb[:, b * 128:(b + 1) * 128], self.cbf("identb", parts=nq)[:, 0:nq], R=(YB, self.CB), W=(self.PS[0],))
        self.cp(OT[:, blk0:blk0 + 2, cols], pv[:, :, 0:nq], R=(self.PS[0],), W=(OT,), eng="act")

    def mix_A(self, l, t, c, U, AZ, SA, OT, NEGA):
        G, PS, cf, sm = self.G, self.PS, self.c, self.sm
        cols = slice(c * 64, (c + 1) * 64)
        self.state_io(l, t, c, SA, "state_a_S", "p_a_S", "s_a_S", True)
        v4 = lambda b: b[0:64, 0:256].rearrange("p (h e) -> p h e", e=64)
        S = self.SSM
        az = AZ[0:64, 0, :]
        BETA, NB, GS, GC, EG = S[0:64, 0:4], S[0:64, 4:8], S[0:64, 8:12], S[0:64, 12:16], G[9][0:64, 0:4]
        EGL = G[9][:, 8:12]
        self.act(BETA, az[:, 256:260], AF.Exp, scale=-1.0, R=(AZ,), W=(S,))
        self.ts(BETA, BETA, 1.0, None, ALU.add, R=(S,), W=(S,))
        self.R.op("dve", lambda e: e.reciprocal(out=BETA, in_=BETA), (S,), (S,))
        self.ts(NB, BETA, -1.0, None, ALU.mult, R=(S,), W=(S,))
        self.tt(GS, az[:, 260:264], sm("dtb", 64)[:, l * 4:(l + 1) * 4], ALU.add, R=(AZ, self.SM), W=(S,))
        self.act(GS, GS, AF.Exp, R=(S,), W=(S,))
        self.act(GS, GS, AF.Ln, bias=cf("ones", 64)[:, 0:1], R=(S, self.CF), W=(S,))
        self.tt(GS, GS, NEGA[0:64, :], ALU.mult, R=(S, self.SM1b), W=(S,))
        tri = cf("tri64", 64)
        pg = self.psv(7, [4], parts=64)
        self.mm(pg, tri, GS, R=(self.CF, S), W=(PS[7],))
        self.cp(GC, pg, R=(PS[7],), W=(S,))
        GB = G[0][0:64, 0:512].rearrange("p (h e) -> p h e", e=128)
        for h in range(4):
            self.ts(GB[:, h, :], cf("ones", 64), GS[:, h:h + 1], None, ALU.mult, R=(self.CF, S), W=(G[0],))
        pw = self.psv(6, [4, 64])
        for h in range(4):
            self.mm(pw[:, h, :], GB[:, h, :], tri, R=(G[0], self.CF), W=(PS[6],))
        GW = G[1][:, 0:256].rearrange("p (h e) -> p h e", e=64)
        self.act(GW, pw, AF.Copy, R=(PS[6],), W=(G[1],))
        self.act(EG, GC, AF.Exp, R=(S,), W=(G[9],))
        self.act(EGL, GW[:, :, 63], AF.Exp, R=(G[1],), W=(G[9],))
        T1, T2 = v4(G[2]), v4(G[3])
        mneg, mpos = v4(Buf(cf("mneg_ui"), "x")), v4(Buf(cf("mpos_sl"), "x"))
        for h in range(4):
            self.stt(T1[:, h, :], GW[0:64, h, :], GC[:, h:h + 1], mneg[:, h, :], ALU.subtract, ALU.min, R=(G[1], S, self.CF), W=(G[2],))
            self.stt(T2[:, h, :], GW[0:64, h, :], GC[:, h:h + 1], mpos[:, h, :], ALU.subtract, ALU.max, R=(G[1], S, self.CF), W=(G[3],))
        self.act(T1, T1, AF.Exp, R=(G[2],), W=(G[2],))
        self.act(T2, T2, AF.Exp, scale=-1.0, R=(G[3],), W=(G[3],))
        pkk, pkq = self.psv(3, [4, 64], parts=64), self.psv(4, [4, 64], parts=64)
        KZ = G[8][:, 0:512].rearrange("p (a h e) -> p a h e", a=2, e=64)
        self.memset(KZ, 0.0, W=(G[8],))
        for h in range(4):
            p, hs = h // 2, slice((h % 2) * 64, (h % 2) * 64 + 64)
            self.cp(KZ[hs, 0, h, :], U[hs, 2 + p, cols], R=(U,), W=(G[8],), eng="pool")
            self.cp(KZ[hs, 1, h, :], U[hs, p, cols], R=(U,), W=(G[8],), eng="pool")
        for h in range(4):
            p = h // 2
            self.mm(pkk[:, h, :], U[:, 2 + p, cols], KZ[:, 0, h, :], R=(U, G[8]), W=(PS[3],))
            self.mm(pkq[:, h, :], U[:, 2 + p, cols], KZ[:, 1, h, :], R=(U, G[8]), W=(PS[4],))
        XM, AQK = v4(G[4]), v4(G[5])
        for h in range(4):
            self.stt(XM[:, h, :], pkk[:, h, :], NB[:, h:h + 1], T2[:, h, :], ALU.mult, ALU.mult, R=(PS[3], S, G[3]), W=(G[4],))
        self.tt(AQK, pkq, T1, ALU.mult, R=(PS[4], G[2]), W=(G[5],))
        i64 = cf("ident", 64)[:, 0:64]
        pn_ = self.psv(5, [4, 64], parts=64)
        for h in range(4):
            self.tr(pn_[:, h, :], XM[:, h, :], i64, R=(G[4], self.CF), W=(PS[5],))
        NM = v4(G[6])
        self.act(NM, pn_, AF.Copy, R=(PS[5],), W=(G[6],))
        RM = v4(G[7])
        i4 = cf("ident64", 64)
        for h in range(4):
            self.tt(RM[:, h, :], NM[:, h, :], i4, ALU.add, R=(G[6], self.CF), W=(G[7],))
        Pb, Qb = [G[6], G[0]], [G[4], G[3]]
        cur = 0
        for k in range(5):
            P, Q = v4(Pb[cur]), v4(Qb[cur])
            Pn, Qn = v4(Pb[1 - cur]), v4(Qb[1 - cur])
            pp = self.psv(3, [8, 64], parts=64)
            for h in range(4):
                self.mm(pp[:, h, :], Q[:, h, :], P[:, h, :], R=(Pb[cur], Qb[cur]), W=(PS[3],))
                self.mm(pp[:, 4 + h, :], P[:, h, :], Q[:, h, :], R=(Pb[cur], Qb[cur]), W=(PS[3],))
            self.act(Pn, pp[:, 0:4, :], AF.Copy, R=(PS[3],), W=(Pb[1 - cur],))
            self.cp(Qn, pp[:, 4:8, :], R=(PS[3],), W=(Qb[1 - cur],))
            pr = self.psv(4, [4, 64], parts=64)
            for h in range(4):
                self.mm(pr[:, h, :], Qn[:, h, :], RM[:, h, :], R=(Qb[1 - cur], G[7]), W=(PS[4],))
            self.tt(RM, RM, pr, ALU.add, R=(G[7], PS[4]), W=(G[7],))
            cur = 1 - cur
        pt = self.psv(5, [4, 128], parts=64)
        for p in range(2):
            self.tr(pt[:, p, :], U[:, 4 + p, cols], cf("ident"), R=(U, self.CF), W=(PS[5],))
            self.tr(pt[:, 2 + p, :], U[:, 2 + p, cols], cf("ident"), R=(U, self.CF), W=(PS[5],))
        VT, KTt = v4(G[0]), v4(G[3])
        self.act(VT, pt[:, 0:2, :].rearrange("p a (b e) -> p (a b) e", e=64), AF.Copy, R=(PS[5],), W=(G[0],))
        self.cp(KTt, pt[:, 2:4, :].rearrange("p a (b e) -> p (a b) e", e=64), R=(PS[5],), W=(G[3],))
        pm1, pqs = self.psv(3, [4, 64], parts=64), self.psv(6, [4, 64], parts=64)
        for h in range(4):
            p, hb = h // 2, (h % 2) * 64
            self.mm(pm1[:, h, :], U[:, 2 + p, cols], SA[:, h, :], R=(U, SA), W=(PS[3],))
            self.mm(pqs[:, h, :], U[:, p, cols], SA[:, h, :], R=(U, SA), W=(PS[6],))
        RW = v4(G[4])
        self.bmul(RW, pm1, EG, 4, R=(PS[3], G[9]), W=(G[4],))
        self.tt(RW, VT, RW, ALU.subtract, R=(G[0], G[4]), W=(G[4],))
        self.bmul(RW, RW, BETA, 4, R=(G[4], S), W=(G[4],))
        pw_ = self.psv(4, [4, 64], parts=64)
        for h in range(4):
            self.mm(pw_[:, h, :], RM[:, h, :], RW[:, h, :], R=(G[7], G[4]), W=(PS[4],))
        WS = v4(G[6])
        self.act(WS, pw_, AF.Copy, R=(PS[4],), W=(G[6],))
        QSG = v4(G[13])
        self.bmul(QSG, pqs, EG, 4, R=(PS[6], G[9]), W=(G[13],))
        po2 = self.psv(3, [4, 64], parts=64)
        for h in range(4):
            self.mm(po2[:, h, :], AQK[:, h, :], WS[:, h, :], R=(G[5], G[6]), W=(PS[3],))
        self.tt(QSG, QSG, po2, ALU.add, R=(G[13], PS[3]), W=(G[13],))
        KDA = v4(G[0])
        for h in range(4):
            self.ts(KDA[:, h, :], KTt[:, h, :], T1[:, h, 63:64], None, ALU.mult, R=(G[3], G[2]), W=(G[0],))
        psn = self.psv(5, [2, 64])
        for h in range(4):
            p, hb = h // 2, (h % 2) * 64
            self.mm(psn[hb:hb + 64, p, :], KDA[:, h, :], WS[:, h, :], R=(G[0], G[6]), W=(PS[5],))
        for h in range(4):
            p, hb = h // 2, (h % 2) * 64
            self.stt(SA[hb:hb + 64, h, :], SA[hb:hb + 64, h, :], EGL[hb:hb + 64, h:h + 1], psn[hb:hb + 64, p, :], ALU.mult, ALU.add,
                     R=(SA, G[9], PS[5]), W=(SA,))
        self.gated_norm_T2(G[13], 64, sm("anorm")[0:64, l * 256:(l + 1) * 256], az[:, 0:256], (AZ,), OT, 0, cols)
        self.state_io(l, t, c, SA, "state_a_S", "p_a_S", "s_a_S", False)


    def _qblocks(self, t):
        ntp = self.cfg["NTP"]
        if t < ntp:
            return [(slice(0, 128), 128, 0, None)]
        return [(slice(s * 64, (s + 1) * 64), 64, s * 64, (t - ntp) * 2 + s) for s in range(2)]

    def _stage(self, l, sq, kt, st, kname, vname, is_b):
        PS = self.PS
        kb, vb = (st["kb"], st["vb"]) if is_b else (st["kc"], st["vc"])
        rows = slice(kt * 128, (kt + 1) * 128)
        self.load(kb, kb[:, :], self.din[kname][l, sq, rows, :], q="pool")
        if is_b:
            self.load(vb, vb[:, :, 0:64], self.din[vname][l, sq, rows, :].rearrange("p (h e) -> p h e", e=64), q="pool")
        else:
            self.load(vb, vb[:, :], self.din[vname][l, sq, rows, :], q="pool")
        pv = self.psv(0, [2, 128], BF16)
        for p in range(2):
            self.tr(pv[:, p, :], kb[:, p * 128:(p + 1) * 128], self.cbf("identb"), R=(kb, self.CB), W=(PS[0],))
        self.cp(st["kt"][:, :, :], pv, R=(PS[0],), W=(st["kt"],), eng="act")
        return st["kt"], vb

    def mix_B(self, l, t, QZ, KTB, VB, KTBs, VBs, stg, OT, NLAM, lam_init):
        G, PS, cf, sm = self.G, self.PS, self.c, self.sm
        ntp, PAST = self.cfg["NTP"], self.cfg["PAST"]
        PTs = [G[0][:, 0:512].bitcast(BF16).rearrange("p (a b) -> p a b", b=128),
               G[6][:, 0:512].bitcast(BF16).rearrange("p (a b) -> p a b", b=128)]
        PTb = [G[0], G[6]]
        SDg = [G[1], G[2]]
        bias = cf("bias")
        for (qc, nq, poff, sq) in self._qblocks(t):
            spss = [[self.psv(3, [4, 128]), self.psv(4, [4, 128])], [self.psv(1, [4, 128]), self.psv(2, [4, 128])]]
            spbk = [[3, 4], [1, 2]]
            acc = [self.psv(5, [4, 65]), self.psv(6, [4, 65])]
            if sq is None:
                tiles = [("far", kt, t - kt) for kt in range(t)] + [("diag", t, 0)]
            else:
                tiles = [("cache", kt, PAST // 128 - kt) for kt in range(PAST // 128)] + [("own", 0, 0)]
            zb = self.cbf("zerob")
            for bk in range(2):
                self.mm(acc[bk][0:nq, :, :], zb[:, 0:nq], zb[:, 0:260], True, False, R=(self.CB,), W=(PS[5 + bk],))
            for ti, (kind, kt, n) in enumerate(tiles):
                first, last = ti == 0, ti == len(tiles) - 1
                sps, sbk, PT, PTB = spss[ti % 2], spbk[ti % 2], PTs[ti % 2], PTb[ti % 2]
                if kind == "far":
                    KT, KTbuf, V, Vbuf = KTB[:, :, kt * 128:(kt + 1) * 128], KTB, VB[:, kt, :, :], VB
                    ks, npk = slice(0, 128), 128
                elif kind == "diag":
                    KT, KTbuf, V, Vbuf = KTB[:, :, t * 128:(t + 1) * 128], KTB, VB[:, t, :, :], VB
                    ks, npk = slice(0, 128), 128
                elif kind == "cache":
                    st = stg[kt % 2]
                    KTbuf, Vbuf = self._stage(l, sq, kt, st, "cache_b_k", "cache_b_v", True)
                    KT, V = KTbuf[:, :, :], Vbuf[:, :, :]
                    ks, npk = slice(0, 128), 128
                else:
                    KT, KTbuf, V, Vbuf = KTBs[:, :, qc], KTBs, VBs[:, poff // 64, :, :], VBs
                    ks, npk = slice(0, 64), 64
                for h in range(4):
                    p, hb = h // 2, (h % 2) * 64
                    for c in range(2):
                        i = h * 2 + c
                        self.mm(sps[i // 4][ks, i % 4, 0:nq], KT[:, p, :], QZ[:, p, (h % 2) * 2 + c, qc],
                                R=(KTbuf, QZ), W=(PS[sbk[i // 4]],))
                if kind in ("far", "cache"):
                    for h in range(4):
                        self.act(PT[:, h * 2:h * 2 + 2, 0:nq], sps[h // 2][:, (h % 2) * 2:(h % 2) * 2 + 2, 0:nq], AF.Exp,
                                 bias=bias[:, h * 33 + n:h * 33 + n + 1], R=(PS[sbk[h // 2]], self.CF), W=(PTB,))
                else:
                    for h in range(4):
                        if kind == "diag":
                            db = cf("diagb")[:, h * 128:(h + 1) * 128]
                        else:
                            db = cf("diags")[ks, h * 64:(h + 1) * 64]
                        for c in range(2):
                            i = h * 2 + c
                            sd = SDg[i // 4][ks, (i % 4) * 128:(i % 4) * 128 + nq]
                            self.tt(sd, sps[i // 4][ks, i % 4, 0:nq], db, ALU.add, R=(PS[sbk[i // 4]], self.CF), W=(SDg[i // 4],))
                    for half in range(2):
                        sdv = SDg[half][ks, 0:512].rearrange("p (a b) -> p a b", b=128)
                        self.act(PT[ks, half * 4:(half + 1) * 4, 0:nq], sdv[:, :, 0:nq], AF.Exp, R=(SDg[half],), W=(PTB,))
                for h in range(4):
                    for c in range(2):
                        i = h * 2 + c
                        self.mm(acc[i // 4][0:nq, i % 4, :], PT[ks, i, 0:nq], V[ks, h, :], False, last and (i % 4 == 3),
                                R=(PTB, Vbuf), W=(PS[5 + i // 4],))
            ACs = G[3][0:nq, 0:520].rearrange("p (a b) -> p a b", b=65) if False else None
            A0 = G[3][0:nq, 0:260].rearrange("p (a b) -> p a b", b=65)
            A1 = G[4][0:nq, 0:260].rearrange("p (a b) -> p a b", b=65)
            self.act(A0, acc[0][0:nq, :, :], AF.Copy, R=(PS[5],), W=(G[3],))
            self.act(A1, acc[1][0:nq, :, :], AF.Copy, R=(PS[6],), W=(G[4],))
            RD = self.SSM
            self.R.op("dve", lambda e, A0=A0, nq=nq: e.reciprocal(out=RD[0:nq, 0:4], in_=A0[:, :, 64]), (G[3],), (RD,))
            self.R.op("dve", lambda e, A1=A1, nq=nq: e.reciprocal(out=RD[0:nq, 4:8], in_=A1[:, :, 64]), (G[4],), (RD,))
            ON = G[5][0:nq, 0:512].rearrange("p (a b) -> p a b", b=64)
            for i in range(8):
                src = (A0 if i < 4 else A1)[:, i % 4, 0:64]
                self.ts(ON[:, i, :], src, RD[0:nq, i:i + 1], None, ALU.mult, R=(G[3], G[4], RD), W=(G[5],))
            ON4 = G[5][0:nq, 0:512].rearrange("p (h c e) -> p h c e", c=2, e=64)
            OB = G[8]
            obv = OB[0:nq, 0:256].rearrange("p (h e) -> p h e", e=64)
            self.stt(obv, ON4[:, :, 1, :], NLAM[0:nq, :], ON4[:, :, 0, :], ALU.mult, ALU.add, R=(G[5], self.SM1b), W=(G[8],))
            self.gated_norm_T2(OB, nq, sm("bnorm")[0:nq, l * 256:(l + 1) * 256], None, (), OT, 2, qc, post_scale=1.0 - lam_init)

    def mix_C(self, l, t, CQ, KTC, VC, KTCs, VCs, stg, OT):
        G, PS, cf = self.G, self.PS, self.c
        ntp, PAST = self.cfg["NTP"], self.cfg["PAST"]
        RR = G[3]
        sets = [dict(EZ=G[0], TT=G[1], EC=G[2], SPB=G[4], ATB=G[5], SPb=G[4][:, 0:256].bitcast(BF16), ATb=G[5][:, 0:256].bitcast(BF16),
                     zb=3, cb=4, rb=5),
                dict(EZ=G[6], TT=G[7], EC=G[8], SPB=G[10], ATB=G[10], SPb=G[10][:, 0:256].bitcast(BF16),
                     ATb=G[10][:, 256:512].bitcast(BF16), zb=7, cb=1, rb=2)]
        utri, onesb = self.cbf("utri"), self.cbf("onesb")
        for (qc, nq, poff, sq) in self._qblocks(t):
            N4 = 4 * nq
            v3 = lambda ap: ap.rearrange("p (h q) -> p h q", q=nq)
            accc = self.psv(6, [4, 64])
            if sq is None:
                tiles = [("diag", t)] + [("far", kt) for kt in range(t - 1, -1, -1)]
            else:
                tiles = [("own", 0)] + [("cache", kt) for kt in range(PAST // 128 - 1, -1, -1)]
            zb = self.cbf("zerob")
            self.mm(accc[0:nq, :, :], zb[:, 0:nq], zb[:, 0:256], True, False, R=(self.CB,), W=(PS[6],))
            for ti, (kind, kt) in enumerate(tiles):
                first, last = ti == 0, ti == len(tiles) - 1
                S_ = sets[ti % 2]
                EZ, TT, EC, SPB, ATB, SPb, ATb = S_["EZ"], S_["TT"], S_["EC"], S_["SPB"], S_["ATB"], S_["SPb"], S_["ATb"]
                zb, cbk, rbk = S_["zb"], S_["cb"], S_["rb"]
                zp, cpv, rpv = self.psv(zb, [4, nq]), self.psv(cbk, [N4]), self.psv(rbk, [N4])
                if kind in ("far", "diag"):
                    KT, KTbuf, V, Vbuf = KTC[:, :, kt * 128:(kt + 1) * 128], KTC, VC[:, kt, :], VC
                    ks = slice(0, 128)
                elif kind == "cache":
                    st = stg[kt % 2]
                    KTbuf, Vbuf = self._stage(l, sq, kt, st, "cache_c_k", "cache_c_v", False)
                    KT, V = KTbuf[:, :, :], Vbuf[:, :]
                    ks = slice(0, 128)
                else:
                    KT, KTbuf, V, Vbuf = KTCs[:, :, qc], KTCs, VCs[:, poff // 64, :], VCs
                    ks = slice(0, 64)
                for h in range(4):
                    p, hb = h // 2, (h % 2) * 64
                    self.mm(zp[ks, h, :], KT[:, p, :], CQ[:, p, h % 2, qc], R=(KTbuf, CQ), W=(PS[zb],))
                ez, sp = EZ[ks, 0:N4], SPb[ks, 0:N4]
                self.act(v3(ez), zp[ks, :, :], AF.Exp, R=(PS[zb],), W=(EZ,))
                if first:
                    spf = TT[ks, 0:N4]
                    self.act(spf, ez, AF.Ln, bias=cf("ones")[ks, 0:1], R=(EZ, self.CF), W=(TT,))
                    cm = cf("cmask")[ks, ks.start:ks.start + nq]
                    for h in range(4):
                        self.tt(v3(sp)[:, h, :], v3(spf)[:, h, :], cm, ALU.mult, R=(TT, self.CF), W=(SPB,))
                        self.tt(v3(ez)[:, h, :], v3(ez)[:, h, :], cm, ALU.mult, R=(EZ, self.CF), W=(EZ,))
                else:
                    self.act(sp, ez, AF.Ln, bias=cf("ones")[ks, 0:1], R=(EZ, self.CF), W=(SPB,))
                self.mm(cpv[ks, :], utri[ks, ks], sp, R=(self.CB, SPB), W=(PS[cbk],))
                self.mm(rpv, onesb[ks, :], sp, R=(self.CB, SPB), W=(PS[rbk],))
                ec = EC[ks, 0:N4]
                if first:
                    self.act(ec, cpv[ks, :], AF.Exp, scale=-1.0, R=(PS[cbk],), W=(EC,))
                    self.cp(RR[:, 0:N4], rpv, R=(PS[rbk],), W=(RR,))
                else:
                    tt_ = TT[ks, 0:N4]
                    self.tt(tt_, cpv[ks, :], RR[ks, 0:N4], ALU.add, R=(PS[cbk], RR), W=(TT,))
                    self.act(ec, tt_, AF.Exp, scale=-1.0, R=(TT,), W=(EC,))
                    if not last:
                        self.tt(RR[:, 0:N4], RR[:, 0:N4], rpv, ALU.add, R=(RR, PS[rbk]), W=(RR,))
                at = ATb[ks, 0:N4]
                self.tt(at, ez, ec, ALU.mult, R=(EZ, EC), W=(ATB,))
                for h in range(4):
                    self.mm(accc[0:nq, h, :], v3(at)[:, h, :], V[ks, h * 64:(h + 1) * 64], False, last and h == 3, R=(ATB, Vbuf), W=(PS[6],))
            YB = G[11]
            yb = YB[0:nq, 256:384].bitcast(BF16)
            self.act(yb.rearrange("p (h e) -> p h e", e=64), accc[0:nq, :, :], AF.Copy, R=(PS[6],), W=(YB,))
            self.to_OT(YB, yb, nq, OT, 4, qc)

    def phase_x(self, l, first_phase):
        cfg = self.cfg
        ntp, nts = cfg["NTP"], cfg["NTS"]
        mark = self.top
        Wcq = self.alloc("Wcq", [8, 1024], BF16)
        Wco = self.alloc("Wco", [8, 1024], BF16)
        Wck = self.alloc("Wck", [8, 1024], BF16)
        Wcv = self.alloc("Wcv", [8, 1024], BF16)
        self.wload(Wck, self.din["w_ck"][l], 8)
        self.wload(Wcv, self.din["w_cv"][l], 8)
        self.wload(Wcq, self.din["w_cq"][l], 8)
        self.wload(Wco, self.din["w_co"][l], 8)
        MKT = self.alloc("MKT", [8, 256], BF16)
        MV = self.alloc("MV", [2, 1024], BF16)
        MHT = self.alloc("MHT", [8, 256], BF16)
        MST = self.alloc("MST", [2, 512])
        smem = []
        for s in range(2):
            smem.append((self.alloc("SKS%d" % s, [2, 1024], BF16), self.alloc("SMKT%d" % s, [8, 256], BF16),
                         self.alloc("SMV%d" % s, [2, 1024], BF16)))
        CQT = self.alloc("CQT", [8, 128], BF16)
        PT = self.alloc("PT", [8, 128], BF16)
        COT = self.alloc("COT", [8, 128], BF16)
        LN = self.alloc("LN", [4, 128])
        RC = self.alloc("RC", [4, 128])
        PS = self.PS
        XT, hT = self.XT, self.hT
        for mt in range(2):
            xb = XT[mt]
            self.load(xb, xb[:, :], self.din["memp"][mt * 128:(mt + 1) * 128, :])
            self.norm_T(xb[:, :], xb, 6 + l, MHT, tcols=slice(mt * 128, (mt + 1) * 128))
        for j in range(8):
            if "nomk" in cfg.get("dbg", ""):
                break
            pv = self.psv(1 + j % 2, [256])
            for k in range(8):
                self.mm(pv, Wck[:, k, j * 128:(j + 1) * 128], MHT[:, k, :], k == 0, k == 7, R=(Wck, MHT), W=(PS[1 + j % 2],))
            self.cp(MKT[:, j, :], pv, R=(PS[1 + j % 2],), W=(MKT,), eng="act" if j % 2 else "dve")
        si = 0
        for (Wm, is_v, oname) in ((Wck, False, "p_mem_k"), (Wcv, True, "p_mem_v")):
            if "nomv" in cfg.get("dbg", ""):
                break
            for mt in range(2):
                for hf in range(2):
                    b = 3 + si % 2
                    pv = self.psv(b, [512])
                    for k in range(8):
                        self.mm(pv, MHT[:, k, mt * 128:(mt + 1) * 128], Wm[:, k, hf * 512:(hf + 1) * 512], k == 0, k == 7,
                                R=(Wm, MHT), W=(PS[b],))
                    self.cp(MST[:, si % 2, :], pv, R=(PS[b],), W=(MST,), eng="act")
                    if is_v:
                        self.cp(MV[:, mt, hf * 512:(hf + 1) * 512], pv, R=(PS[b],), W=(MV,))
                    self.store(MST, self.dout[oname][l, mt * 128:(mt + 1) * 128, hf * 512:(hf + 1) * 512], MST[:, si % 2, :])
                    si += 1
        def ldx(t):
            src = self.xin_rows(t) if first_phase else self.xrows(t)
            self.load(XT[t % 2], XT[t % 2][:, :], src)
        ldx(0)
        for t in range(ntp + nts):
            if "notiles" in cfg.get("dbg", ""):
                break
            xb = XT[t % 2]
            if t + 1 < ntp + nts:
                ldx(t + 1)
            if t < ntp or "nosamp" in cfg.get("dbg", ""):
                segs = [(slice(0, 128), MKT, MV)]
            else:
                segs = []
                for s in range(2):
                    KS, SMKT, SMV = smem[s]
                    sq = (t - ntp) * 2 + s
                    self.load(KS, KS[:, :, :], self.din["cache_mem_k"][l, sq].rearrange("(a p) d -> p a d", p=128), q="pool")
                    self.load(SMV, SMV[:, :, :], self.din["cache_mem_v"][l, sq].rearrange("(a p) d -> p a d", p=128), q="pool")
                    for j in range(8):
                        b = 1 + j % 2
                        pv = self.psv(b, [256], BF16)
                        for mt in range(2):
                            self.tr(pv[:, mt * 128:(mt + 1) * 128], KS[:, mt, j * 128:(j + 1) * 128], self.cbf("identb"),
                                    R=(KS, self.CB), W=(PS[b],))
                        self.cp(SMKT[:, j, :], pv, R=(PS[b],), W=(SMKT,), eng="act" if j % 2 else "dve")
                    segs.append((slice(s * 64, (s + 1) * 64), SMKT, SMV))
            self.norm_T(xb[:, :], xb, 2 + l, hT)
            for half in range(2):
                b = 1 + half
                pv = self.psv(b, [4, 128])
                for jj in range(4):
                    j = half * 4 + jj
                    for k in range(8):
                        self.mm(pv[:, jj, :], Wcq[:, k, j * 128:(j + 1) * 128], hT[:, k, :], k == 0, k == 7, R=(Wcq, hT), W=(PS[b],))
                self.act(CQT[:, half * 4:(half + 1) * 4, :], pv, AF.Copy, scale=1.0 / 16.0, R=(PS[b],), W=(CQT,))
            for (cs, mkt, mv) in segs:
                for h in range(4):
                    b = 3 + h // 2
                    pv = self.psv(b, [4, 128])
                    for mt in range(2):
                        for c in range(2):
                            self.mm(pv[:, (h % 2) * 2 + mt, cs], mkt[:, h * 2 + c, mt * 128:(mt + 1) * 128], CQT[:, h * 2 + c, cs],
                                    c == 0, c == 1, R=(mkt, CQT), W=(PS[b],))
            for half in range(2):
                self.act(PT[:, half * 4:(half + 1) * 4, :], self.psv(3 + half, [4, 128]), AF.Exp, R=(PS[3 + half],), W=(PT,))
            for (cs, mkt, mv) in segs:
                for h in range(4):
                    b = 5 + h // 2
                    pv = self.psv(b, [4, 128])
                    for c in range(2):
                        for mt in range(2):
                            self.mm(pv[:, (h % 2) * 2 + c, cs], mv[:, mt, h * 256 + c * 128:h * 256 + (c + 1) * 128],
                                    PT[:, h * 2 + mt, cs], mt == 0, mt == 1, R=(mv, PT), W=(PS[b],))
                    pn = self.psv(7, [4, 128])
                    for mt in range(2):
                        self.mm(pn[:, h, cs], self.cbf("onesb"), PT[:, h * 2 + mt, cs], mt == 0, mt == 1, R=(self.CB, PT), W=(PS[7],))
            self.act(LN[:, :, :], self.psv(7, [4, 128]), AF.Ln, R=(PS[7],), W=(LN,))
            self.act(RC[:, :, :], LN[:, :, :], AF.Exp, scale=-1.0, R=(LN,), W=(RC,))
            for h in range(4):
                b = 5 + h // 2
                pv = self.psv(b, [4, 128])
                for c in range(2):
                    self.tt(COT[:, h * 2 + c, :], pv[:, (h % 2) * 2 + c, :], RC[:, h, :], ALU.mult, R=(PS[b], RC), W=(COT,))
            for hf in range(2):
                b = 1 + hf
                pv = self.psv(b, [512])
                for j in range(8):
                    self.mm(pv, COT[:, j, :], Wco[:, j, hf * 512:(hf + 1) * 512], j == 0, j == 7, R=(COT, Wco), W=(PS[b],))
                self.tt(xb[:, hf * 512:(hf + 1) * 512], xb[:, hf * 512:(hf + 1) * 512], pv, ALU.add, R=(PS[b], xb), W=(xb,))
            self.store(xb, self.xrows(t), xb[:, :])
        self.R.barrier()
        self.top = mark

    def phase_f(self, l, last):
        cfg = self.cfg
        ntp, nts = cfg["NTP"], cfg["NTS"]
        mark = self.top
        Wg = self.alloc("Wg", [8, DFF], BF16)
        Wu = self.alloc("Wu", [8, DFF], BF16)
        Wd = self.alloc("Wd", [22, 1024], BF16)
        self.wload(Wg, self.din["w_gate"][l], 8)
        self.wload(Wu, self.din["w_up"][l], 8)
        self.wload(Wd, self.din["w_down"][l], 22)
        AT = self.alloc("AT", [22, 128], BF16)
        SG = [self.alloc("SG%d" % i, [4, 128]) for i in range(2)]
        if last:
            NF = self.alloc("NF", [1024])
            self.load(NF, NF[:, :], self.din["nfin"][:, :])
        PS = self.PS
        XT, hT = self.XT, self.hT
        self.load(XT[0], XT[0][:, :], self.xrows(0))
        for t in range(ntp + nts):
            xb = XT[t % 2]
            if t + 1 < ntp + nts:
                self.load(XT[(t + 1) % 2], XT[(t + 1) % 2][:, :], self.xrows(t + 1))
            self.norm_T(xb[:, :], xb, 4 + l, hT)
            for gi in range(6):
                j0 = gi * 4
                nb = min(4, 22 - j0)
                bg, bu = 1 + 2 * (gi % 2), 2 + 2 * (gi % 2)
                pg, pu = self.psv(bg, [4, 128]), self.psv(bu, [4, 128])
                for jj in range(nb):
                    j = j0 + jj
                    for k in range(8):
                        self.mm(pg[:, jj, :], Wg[:, k, j * 128:(j + 1) * 128], hT[:, k, :], k == 0, k == 7, R=(Wg, hT), W=(PS[bg],))
                    for k in range(8):
                        self.mm(pu[:, jj, :], Wu[:, k, j * 128:(j + 1) * 128], hT[:, k, :], k == 0, k == 7, R=(Wu, hT), W=(PS[bu],))
                sg = SG[gi % 2]
                self.act(sg[:, 0:nb, :], pg[:, 0:nb, :], AF.Silu, R=(PS[bg],), W=(sg,))
                self.tt(AT[:, j0:j0 + nb, :], sg[:, 0:nb, :], pu[:, 0:nb, :], ALU.mult, R=(sg, PS[bu]), W=(AT,))
            for hf in range(2):
                b = 5 + hf
                pv = self.psv(b, [512])
                for j in range(22):
                    self.mm(pv, AT[:, j, :], Wd[:, j, hf * 512:(hf + 1) * 512], j == 0, j == 21, R=(AT, Wd), W=(PS[b],))
                self.tt(xb[:, hf * 512:(hf + 1) * 512], xb[:, hf * 512:(hf + 1) * 512], pv, ALU.add, R=(PS[b], xb), W=(xb,))
            if last:
                XN, SS = self.XN, self.SS
                self.act(XN[:, :], xb[:, :], AF.Square, R=(xb,), W=(XN, SS), accum=SS[:, 0:1])
                self.act(SS[:, 1:2], SS[:, 0:1], AF.Ln, bias=self.epsb[:, 0:1], scale=1.0 / D, R=(SS, self.CF), W=(SS,))
                self.act(SS[:, 2:3], SS[:, 1:2], AF.Exp, scale=-0.5, R=(SS,), W=(SS,))
                self.stt(xb[:, :], xb[:, :], SS[:, 2:3], NF[:, :], ALU.mult, ALU.mult, R=(xb, SS, NF), W=(xb,))
            self.store(xb, self.xrows(t), xb[:, :])
        self.R.barrier()
        self.top = mark

    def declare_io(self):
        cfg = self.cfg
        SEQ, PAST, NSQ = cfg["NTP"] * 128, cfg["PAST"], cfg["NTS"] * 2
        i, o = self.inp, self.outp
        i("xp", [SEQ, D]); i("xs", [NSQ * 64, D]); i("memp", [256, D])
        i("state_a_conv", [2, NSQ, 3, 768]); i("state_a_S", [2, NSQ, 4, 64, 64])
        for n in ("cache_b_k", "cache_b_v", "cache_c_k", "cache_c_v"):
            i(n, [2, NSQ, PAST, 256])
        i("state_d_S", [2, NSQ, 4, 64, 64])
        i("cache_mem_k", [2, NSQ, 256, D]); i("cache_mem_v", [2, NSQ, 256, D])
        i("w_in", [2, D, DIN]); i("w_out", [2, D, D])
        for n in ("w_cq", "w_ck", "w_cv", "w_co"):
            i(n, [2, D, D])
        i("w_gate", [2, D, DFF]); i("w_up", [2, D, DFF]); i("w_down", [2, DFF, D])
        i("nfin", [128, D])
        o("y_p", [SEQ, D]); o("y_s", [NSQ * 64, D])
        o("p_a_conv", [2, 3, 768]); o("p_a_S", [2, 4, 64, 64])
        for n in ("p_b_k", "p_b_v", "p_c_k", "p_c_v"):
            o(n, [2, SEQ, 256])
        o("p_d_S", [2, 4, 64, 64]); o("p_mem_k", [2, 256, D]); o("p_mem_v", [2, 256, D])
        o("s_a_conv", [2, NSQ, 3, 768]); o("s_a_S", [2, NSQ, 4, 64, 64])
        for n in ("s_b_k", "s_b_v", "s_c_k", "s_c_v"):
            o(n, [2, NSQ, 64, 256])
        o("s_d_S", [2, NSQ, 4, 64, 64])

    def build(self):
        import contextlib
        cfg = self.cfg
        with contextlib.ExitStack() as st:
            self.declare_io()
            self.init_mem(st)
            self.load_consts()
            self.XT = [self.alloc("XT%d" % i, [1024]) for i in range(2)]
            self.XN = self.alloc("XN", [1024], BF16)
            self.SS = self.alloc("SS", [4])
            self.hT = self.alloc("hT", [8, 128], BF16)
            phases = cfg.get("phases", "MXF")
            first = True
            for l in range(2):
                if "M" in phases:
                    self.phase_m(l, first)
                    first = False
                if "X" in phases:
                    self.phase_x(l, first)
                    first = False
                if "F" in phases:
                    self.phase_f(l, l == 1)
            self.R.replay()
        return self.nc


FULL_CFG = dict(NTP=32, NTS=2, PAST=4096, phases="MXF")


def make_in_maps(inputs, cfg, ncores=NCORES):
    nsq = cfg["NTS"] * 2
    SEQ = cfg["NTP"] * 128
    cf, cb = K.host_consts()
    constf = np.ascontiguousarray(np.concatenate(list(cf.values()), axis=1).astype(np.float32))
    constb = np.ascontiguousarray(np.concatenate(list(cb.values()), axis=1).astype(np.float32))
    small = K.host_small(inputs)
    nb = inputs["x_prompt"].shape[0]
    maps = []
    f = lambda a: np.ascontiguousarray(np.asarray(a, dtype=np.float32))
    for c in range(ncores):
        b = c % nb
        sl = slice(c * nsq, (c + 1) * nsq)
        m = {"constf": constf, "constb": constb, "smallp": small,
             "nfin": np.ascontiguousarray(np.broadcast_to(np.asarray(inputs["norm_final"], dtype=np.float32)[None, :], (128, D))),
             "xp": f(inputs["x_prompt"][b]), "xs": f(inputs["x_sample"][sl].reshape(nsq * 64, D)),
             "memp": f(inputs["mem_prompt"][b]),
             "state_a_conv": f(inputs["state_a_conv"][:, sl]), "state_a_S": f(inputs["state_a_S"][:, sl]),
             "state_d_S": f(inputs["state_d_S"][:, sl]),
             "cache_mem_k": f(inputs["cache_mem_k"][:, sl].reshape(2, nsq, 256, D)),
             "cache_mem_v": f(inputs["cache_mem_v"][:, sl].reshape(2, nsq, 256, D))}
        for n in ("cache_b_k", "cache_b_v", "cache_c_k", "cache_c_v"):
            m[n] = f(inputs[n][:, sl].reshape(2, nsq, -1, 256))
        for n in ("w_in", "w_out", "w_cq", "w_ck", "w_cv", "w_co", "w_gate", "w_up", "w_down"):
            m[n] = f(inputs[n])
        maps.append(m)
    return maps


def gather(results, cfg, nb, ncores=NCORES):
    nsq = cfg["NTS"] * 2
    SEQ = cfg["NTP"] * 128
    r = results
    P = lambda n: np.stack([r[b][n] for b in range(nb)], axis=0)
    Sx = lambda n: np.concatenate([r[c][n] for c in range(ncores)], axis=1)
    y_p = P("y_p")
    y_s = np.concatenate([r[c]["y_s"].reshape(nsq, 64, D) for c in range(ncores)], axis=0)
    mv = lambda a: np.moveaxis(a, 0, 1)
    outs = [y_p, y_s, mv(P("p_a_conv")), mv(P("p_a_S"))]
    for n in ("p_b_k", "p_b_v", "p_c_k", "p_c_v"):
        outs.append(mv(P(n)).reshape(2, nb, SEQ, 4, 64))
    outs.append(mv(P("p_d_S")))
    for n in ("p_mem_k", "p_mem_v"):
        outs.append(mv(P(n)).reshape(2, nb, 256, 4, 256))
    outs += [Sx("s_a_conv"), Sx("s_a_S")]
    for n in ("s_b_k", "s_b_v", "s_c_k", "s_c_v"):
        outs.append(Sx(n).reshape(2, nsq * ncores, 64, 4, 64))
    outs.append(Sx("s_d_S"))
    return tuple(np.ascontiguousarray(o.astype(np.float32)) for o in outs)


def kernel(**inputs):
    cfg = FULL_CFG
    inputs = {k: np.asarray(v) for k, v in inputs.items()}
    kb = K(cfg)
    nc = kb.build()
    maps = make_in_maps(inputs, cfg)
    res = run_bass_kernel_spmd(nc, maps, core_ids=list(range(NCORES)))
    return gather(res.results, cfg, inputs["x_prompt"].shape[0])
```
